# Optimizing a Trainium2 kernel written in Bass

```python
import math
import jax, jax.numpy as jnp
from jax import lax
import numpy as np

D_MODEL = 2048
BATCH = 4
SEQ = 2048
DEPTH = 1
DEC_BATCH = 8
DEC_SEQ = 1
PAST_LEN = 16384
PAGE_SIZE = 128

ATT_WIDTH = D_MODEL // 2
CONV_WIDTH = D_MODEL - ATT_WIDTH
HEAD_DIM = 64
QK_DIM = 2 * HEAD_DIM
V_DIM = 2 * HEAD_DIM
N_HEADS = ATT_WIDTH // V_DIM
ROT_DIM = HEAD_DIM // 4
ROPE_THETA = 500000.0
CONV_KERNEL = 31
PLE_DIM = 256
Q_BLOCK = 128
EPS = 1e-6
D_IN = 4 * ATT_WIDTH + 3 * CONV_WIDTH
IN_SPLITS = (ATT_WIDTH, 2 * ATT_WIDTH, 3 * ATT_WIDTH, 4 * ATT_WIDTH, 4 * ATT_WIDTH + 2 * CONV_WIDTH)
NEG_INF = -1e30

kernel_name = "hymba_diffattn_conformer_decode_step"


def rms_norm(x, g):
    xf = x.astype(jnp.float32)
    y = xf * lax.rsqrt(jnp.mean(xf * xf, axis=-1, keepdims=True) + EPS)
    return (y * g.astype(jnp.float32)).astype(x.dtype)


def layer_norm(x, g, b):
    xf = x.astype(jnp.float32)
    mu = jnp.mean(xf, axis=-1, keepdims=True)
    var = jnp.mean(jnp.square(xf - mu), axis=-1, keepdims=True)
    y = (xf - mu) * lax.rsqrt(var + EPS)
    return (y * g.astype(jnp.float32) + b.astype(jnp.float32)).astype(x.dtype)


def partial_rope(x, pos):
    inv = ROPE_THETA ** (-jnp.arange(0, ROT_DIM, 2, dtype=jnp.float32) / ROT_DIM)
    ang = pos.astype(jnp.float32)[:, None] * inv[None, :]
    cos = jnp.cos(ang)[None, :, None, None, :]
    sin = jnp.sin(ang)[None, :, None, None, :]
    xr = x[..., :ROT_DIM].astype(jnp.float32)
    x1, x2 = xr[..., : ROT_DIM // 2], xr[..., ROT_DIM // 2:]
    rot = jnp.concatenate([x1 * cos - x2 * sin, x2 * cos + x1 * sin], axis=-1).astype(x.dtype)
    return jnp.concatenate([rot, x[..., ROT_DIM:]], axis=-1)


def diff_attend(q, k, v, q_pos, k_pos, lam):
    s = jnp.einsum("bqhmd,bkhmd->bhmqk", q, k).astype(jnp.float32) * (HEAD_DIM ** -0.5)
    mask = k_pos[None, :] <= q_pos[:, None]
    s = jnp.where(mask, s, NEG_INF)
    p = jax.nn.softmax(s, axis=-1)
    a = p[:, :, 0] - lam * p[:, :, 1]
    return jnp.einsum("bhqk,bkhe->bqhe", a.astype(v.dtype), v)


def prompt_attention(q, k, v, pos, lam):
    b, s = q.shape[0], q.shape[1]
    nqb = s // Q_BLOCK
    qb = q.reshape(b, nqb, Q_BLOCK, N_HEADS, 2, HEAD_DIM).transpose(1, 0, 2, 3, 4, 5)
    pb = pos.reshape(nqb, Q_BLOCK)
    out = lax.map(lambda xs: diff_attend(xs[0], k, v, xs[1], pos, lam), (qb, pb))
    return out.transpose(1, 0, 2, 3, 4).reshape(b, s, N_HEADS, V_DIM)


def causal_dwconv(hist, w_dw, b_dw):
    y = lax.conv_general_dilated(hist, w_dw[:, None, :].astype(hist.dtype), window_strides=(1,),
                                 padding="VALID", dimension_numbers=("NWC", "WIO", "NWC"),
                                 feature_group_count=CONV_WIDTH)
    return y + b_dw


def mixer_layer(h, p_l, pos, conv_hist, attn_fn, layer_idx, w_in, g_norm, lam_q1, lam_k1, lam_q2, lam_k2,
                g_subln, w_dw, b_dw, g_cln, b_cln, w_pw, w_out, g_ple, w_pg, w_ple):
    b, s, _ = h.shape
    u = rms_norm(h, g_norm)
    z = u @ w_in
    q, k, v, gate_a, cu, gate_c = jnp.split(z, IN_SPLITS, axis=-1)
    q = partial_rope(q.reshape(b, s, N_HEADS, 2, HEAD_DIM), pos)
    k = partial_rope(k.reshape(b, s, N_HEADS, 2, HEAD_DIM), pos)
    v = v.reshape(b, s, N_HEADS, V_DIM)
    lam_init = 0.8 - 0.6 * math.exp(-0.3 * layer_idx)
    lam = (jnp.exp(jnp.sum(lam_q1.astype(jnp.float32) * lam_k1.astype(jnp.float32)))
           - jnp.exp(jnp.sum(lam_q2.astype(jnp.float32) * lam_k2.astype(jnp.float32))) + lam_init)
    o = attn_fn(q, k, v, lam)
    o = rms_norm(o, g_subln) * (1.0 - lam_init)
    o_att = o.reshape(b, s, ATT_WIDTH) * jax.nn.silu(gate_a)
    ca, cb = jnp.split(cu, 2, axis=-1)
    g = ca * jax.nn.sigmoid(cb)
    hist = jnp.concatenate([conv_hist.astype(g.dtype), g], axis=1)
    c = causal_dwconv(hist, w_dw, b_dw)
    c = jax.nn.silu(layer_norm(c, g_cln, b_cln)) @ w_pw
    o_conv = c * jax.nn.silu(gate_c)
    h = h + jnp.concatenate([o_att, o_conv], axis=-1) @ w_out
    gate = jax.nn.sigmoid(rms_norm(h, g_ple) @ w_pg)
    h = h + gate * (p_l @ w_ple)
    new_conv = hist[:, -(CONV_KERNEL - 1):]
    return h, k.reshape(b, s, N_HEADS, QK_DIM), v, new_conv


def setup_inputs(seed: int = 0) -> dict:
    key = jax.random.key(seed)
    ks = jax.random.split(key, 32)
    n_pages = PAST_LEN // PAGE_SIZE
    n_used = DEC_BATCH * n_pages
    n_pool = n_used + n_used // 4
    f32 = jnp.float32
    nrm = lambda k, shp, sc: jax.random.normal(k, shp, f32) * sc
    perm = jax.random.permutation(ks[0], n_pool)[:n_used]
    return {
        "x_prompt": nrm(ks[1], (BATCH, SEQ, D_MODEL), 1.0),
        "x_sample": nrm(ks[2], (DEC_BATCH, DEC_SEQ, D_MODEL), 1.0),
        "p_prompt": nrm(ks[3], (DEPTH, BATCH, SEQ, PLE_DIM), 1.0),
        "p_sample": nrm(ks[4], (DEPTH, DEC_BATCH, DEC_SEQ, PLE_DIM), 1.0),
        "cache_k": nrm(ks[5], (DEPTH, n_pool, PAGE_SIZE, N_HEADS, QK_DIM), 1.0),
        "cache_v": nrm(ks[6], (DEPTH, n_pool, PAGE_SIZE, N_HEADS, V_DIM), 1.0),
        "state_conv": nrm(ks[7], (DEPTH, DEC_BATCH, CONV_KERNEL - 1, CONV_WIDTH), 1.0),
        "page_table": perm.reshape(DEC_BATCH, n_pages).astype(jnp.int32),
        "w_in": nrm(ks[8], (DEPTH, D_MODEL, D_IN), D_MODEL ** -0.5),
        "g_norm": 1.0 + nrm(ks[9], (DEPTH, D_MODEL), 0.02),
        "lam_q1": nrm(ks[10], (DEPTH, HEAD_DIM), 0.1),
        "lam_k1": nrm(ks[11], (DEPTH, HEAD_DIM), 0.1),
        "lam_q2": nrm(ks[12], (DEPTH, HEAD_DIM), 0.1),
        "lam_k2": nrm(ks[13], (DEPTH, HEAD_DIM), 0.1),
        "g_subln": 1.0 + nrm(ks[14], (DEPTH, V_DIM), 0.02),
        "w_dw": nrm(ks[15], (DEPTH, CONV_KERNEL, CONV_WIDTH), CONV_KERNEL ** -0.5),
        "b_dw": nrm(ks[16], (DEPTH, CONV_WIDTH), 0.02),
        "g_cln": 1.0 + nrm(ks[17], (DEPTH, CONV_WIDTH), 0.02),
        "b_cln": nrm(ks[18], (DEPTH, CONV_WIDTH), 0.02),
        "w_pw": nrm(ks[19], (DEPTH, CONV_WIDTH, CONV_WIDTH), CONV_WIDTH ** -0.5),
        "w_out": nrm(ks[20], (DEPTH, ATT_WIDTH + CONV_WIDTH, D_MODEL), (ATT_WIDTH + CONV_WIDTH) ** -0.5),
        "g_ple": 1.0 + nrm(ks[21], (DEPTH, D_MODEL), 0.02),
        "w_pg": nrm(ks[22], (DEPTH, D_MODEL, D_MODEL), D_MODEL ** -0.5),
        "w_ple": nrm(ks[23], (DEPTH, PLE_DIM, D_MODEL), PLE_DIM ** -0.5),
        "g_final": 1.0 + nrm(ks[24], (D_MODEL,), 0.02),
    }


def reference(x_prompt, x_sample, p_prompt, p_sample, cache_k, cache_v, state_conv, page_table,
              w_in, g_norm, lam_q1, lam_k1, lam_q2, lam_k2, g_subln, w_dw, b_dw, g_cln, b_cln,
              w_pw, w_out, g_ple, w_pg, w_ple, g_final):
    n_pages = PAST_LEN // PAGE_SIZE
    past = n_pages * PAGE_SIZE
    pos_p = jnp.arange(SEQ, dtype=jnp.int32)
    pos_s = PAST_LEN + jnp.arange(DEC_SEQ, dtype=jnp.int32)
    k_pos_s = jnp.arange(past + DEC_SEQ, dtype=jnp.int32)
    h_p, h_s = x_prompt, x_sample
    nk_p, nv_p, nc_p, nk_s, nv_s, nc_s = [], [], [], [], [], []
    for i in range(DEPTH):
        params = (w_in[i], g_norm[i], lam_q1[i], lam_k1[i], lam_q2[i], lam_k2[i], g_subln[i],
                  w_dw[i], b_dw[i], g_cln[i], b_cln[i], w_pw[i], w_out[i], g_ple[i], w_pg[i], w_ple[i])

        def prompt_attn(q, k, v, lam):
            return prompt_attention(q, k, v, pos_p, lam)

        ck, cv = cache_k[i], cache_v[i]

        def sample_attn(q, k, v, lam, ck=ck, cv=cv):
            pk = ck[page_table].reshape(DEC_BATCH, past, N_HEADS, 2, HEAD_DIM).astype(k.dtype)
            pv = cv[page_table].reshape(DEC_BATCH, past, N_HEADS, V_DIM).astype(v.dtype)
            k_all = jnp.concatenate([pk, k], axis=1)
            v_all = jnp.concatenate([pv, v], axis=1)
            return diff_attend(q, k_all, v_all, pos_s, k_pos_s, lam)

        zero_hist = jnp.zeros((BATCH, CONV_KERNEL - 1, CONV_WIDTH), x_prompt.dtype)
        h_p, kp, vp, cp = mixer_layer(h_p, p_prompt[i], pos_p, zero_hist, prompt_attn, i, *params)
        h_s, ks_, vs_, cs_ = mixer_layer(h_s, p_sample[i], pos_s, state_conv[i], sample_attn, i, *params)
        nk_p.append(kp); nv_p.append(vp); nc_p.append(cp)
        nk_s.append(ks_); nv_s.append(vs_); nc_s.append(cs_)
    y_prompt = rms_norm(h_p, g_final)
    y_sample = rms_norm(h_s, g_final)
    new_k_prompt = jnp.stack(nk_p)
    new_v_prompt = jnp.stack(nv_p)
    new_conv_prompt = jnp.stack(nc_p)
    new_k_sample = jnp.stack(nk_s)
    new_v_sample = jnp.stack(nv_s)
    new_conv_sample = jnp.stack(nc_s)
    return (y_prompt, y_sample, new_k_prompt, new_v_prompt, new_conv_prompt, new_k_sample, new_v_sample, new_conv_sample)
```

```python
import math
from contextlib import ExitStack

import numpy as np
import concourse.bass as bass
import concourse.mybir as mybir
from concourse.bass_utils import run_bass_kernel_spmd

F32 = mybir.dt.float32
BF16 = mybir.dt.bfloat16
I32 = mybir.dt.int32
ALU = mybir.AluOpType
AF = mybir.ActivationFunctionType
AX = mybir.AxisListType

D = 2048
S_OWN = 1024
NT = 8
NS = 8
DIN = 7168
EPS = 1e-6
LAM_INIT = 0.8 - 0.6 * math.exp(-0.3 * 0)
PAST = 16384
ROPE_THETA = 500000.0
NEG = -30000.0


class Tok:
    __slots__ = ("sem", "val", "eng")

    def __init__(self, sem, val, eng):
        self.sem, self.val, self.eng = sem, val, eng


class Sched:
    NQ = 8
    NQP = 4

    def __init__(self, nc):
        self.nc = nc
        self.eng = {"pe": nc.tensor, "act": nc.scalar, "dve": nc.vector, "pool": nc.gpsimd, "sp": nc.sync}
        self.sem = {e: nc.alloc_semaphore(f"sem_{e}") for e in ("pe", "act", "dve", "pool")}
        self.cnt = {e: 0 for e in self.sem}
        self.nq = {"sp": self.NQ, "pool": self.NQP, "act": 2}
        self.dsem = {q: [nc.alloc_semaphore(f"dsem_{q}_{i}") for i in range(self.nq[q])] for q in ("sp", "pool", "act")}
        self.dcnt = {q: 0 for q in self.dsem}
        self.waited = {}
        self.lastw = {}
        self.readers = {}
        self.nwaits = 0

    def wait(self, e, tok):
        if tok is None:
            return
        if tok.eng == "pe" and e == "pe":
            return
        k = (e, id(tok.sem))
        if self.waited.get(k, 0) >= tok.val:
            return
        self.waited[k] = tok.val
        self.eng[e].wait_ge(tok.sem, tok.val)
        self.nwaits += 1

    def _deps(self, e, reads, writes):
        deps = []
        for k in reads:
            t = self.lastw.get(k)
            if t is not None:
                deps.append(t)
        for k in writes:
            t = self.lastw.get(k)
            if t is not None:
                deps.append(t)
            deps.extend(self.readers.get(k, {}).values())
        deps.sort(key=lambda t: -t.val)
        for t in deps:
            self.wait(e, t)

    def _record(self, tok, reads, writes):
        for k in reads:
            d = self.readers.setdefault(k, {})
            d[id(tok.sem)] = tok
        for k in writes:
            self.lastw[k] = tok
            self.readers[k] = {}

    def op(self, e, fn, reads=(), writes=()):
        self._deps(e, reads, writes)
        ins = fn(self.eng[e])
        self.cnt[e] += 1
        ins.then_inc(self.sem[e], 1)
        tok = Tok(self.sem[e], self.cnt[e], e)
        self._record(tok, reads, writes)
        return tok

    def group(self, e, fns, reads=(), writes=()):
        self._deps(e, reads, writes)
        ins = None
        for fn in fns:
            ins = fn(self.eng[e])
        self.cnt[e] += 1
        ins.then_inc(self.sem[e], 1)
        tok = Tok(self.sem[e], self.cnt[e], e)
        self._record(tok, reads, writes)
        return tok

    def dma(self, q, fn, reads=(), writes=()):
        i = self.dcnt[q]
        nq = self.nq[q]
        slot, rnd = i % nq, i // nq
        sem = self.dsem[q][slot]
        if rnd > 0:
            self.wait(q, Tok(sem, 16 * rnd, "dma"))
        self._deps(q, reads, writes)
        ins = fn(self.eng[q])
        ins.then_inc(sem, 16)
        self.dcnt[q] += 1
        tok = Tok(sem, 16 * (rnd + 1), "dma")
        self._record(tok, reads, writes)
        return tok

    def all_tokens(self):
        toks = [Tok(self.sem[e], self.cnt[e], e) for e in self.sem if self.cnt[e] > 0]
        for q in self.dsem:
            n = self.dcnt[q]
            nq = self.nq[q]
            for s in range(nq):
                r = (n - s + nq - 1) // nq
                if r > 0:
                    toks.append(Tok(self.dsem[q][s], 16 * r, "dma"))
        return toks

    def barrier(self, engines=("pe", "act", "dve", "pool", "sp")):
        toks = self.all_tokens()
        for e in engines:
            for t in toks:
                if t.eng == e and e != "pe":
                    pass
                self.wait(e, t)
        self.lastw.clear()
        self.readers.clear()


def _rope_tables(pos):
    inv = ROPE_THETA ** (-np.arange(0, 16, 2, dtype=np.float32) / 16.0)
    ang = pos.astype(np.float32)[:, None] * inv[None, :].astype(np.float32)
    return np.cos(ang).astype(np.float32), np.sin(ang).astype(np.float32)


def build(stage=99, debug=False, sub=0):
    nc = bass.Bass("TRN2", target_bir_lowering=False)
    dt = nc.dram_tensor

    def din(name, shape, dtype=F32):
        return dt(name, list(shape), dtype, kind="ExternalInput").ap()

    def dout(name, shape, dtype=F32):
        return dt(name, list(shape), dtype, kind="ExternalOutput").ap()

    x_ctx = din("x_ctx", [S_OWN, D])
    x_own = din("x_own", [S_OWN, D])
    x_s = din("x_s", [NS, D])
    p_own = din("p_own", [S_OWN, 256])
    p_s = din("p_s", [NS, 256])
    w_in = din("w_in", [D, DIN])
    w_pw = din("w_pw", [1024, 1024])
    w_out = din("w_out", [D, D])
    w_pg = din("w_pg", [D, D])
    w_ple = din("w_ple", [256, D])
    cache_k4 = din("cache_k4", [1280 * 32, 4096])
    cache_v8 = din("cache_v8", [1280 * 64, 2048])
    pt_T = din("pt_T", [128, 1], I32)
    state_conv = din("state_conv", [NS, 30, 1024])
    g_norm_pc = din("g_norm_pc", [128, 16])
    g_ple_pc = din("g_ple_pc", [128, 16])
    g_final_bc = din("g_final_bc", [128, D])
    g_subln_pc = din("g_subln_pc", [128, 1])
    g_subln_row = din("g_subln_row", [1, 128])
    wdw_pc = din("wdw_pc", [128, 8, 31])
    vec_pc = din("vec_pc", [128, 3, 8])
    lam4 = din("lam4", [128, 4, 64])
    cos_t = din("cos_t", [128, 17, 8])
    sin_t = din("sin_t", [128, 17, 8])
    ident_b = din("ident_b", [128, 128], BF16)
    ident_f = din("ident_f", [128, 128])
    tri_b = din("tri_b", [128, 128], BF16)
    ctx_bias = din("ctx_bias", [128, 1])
    sel2 = din("sel2", [2, 2])

    o_y = dout("o_y", [S_OWN, D])
    o_ys = dout("o_ys", [NS, D])
    o_k = dout("o_k", [S_OWN, 1024])
    o_v = dout("o_v", [S_OWN, 1024])
    o_conv = dout("o_conv", [30, 1024])
    o_ks = dout("o_ks", [NS, 1024])
    o_vs = dout("o_vs", [NS, 1024])
    o_convs = dout("o_convs", [NS, 30, 1024])

    scr_q = dt("scr_q", [1, 1024], F32)

    NTOK = S_OWN + NS

    with ExitStack() as es:
        def sbx(stack, name, shape, dtype=F32):
            return stack.enter_context(nc.sbuf_tensor(name, list(shape), dtype))

        def sb(name, shape, dtype=F32):
            return sbx(es, name, shape, dtype)

        S = Sched(nc)

        identb = sb("identb", [128, 128], BF16)
        identf = sb("identf", [128, 128], F32)
        trib = sb("trib", [128, 128], BF16)
        onesb = sb("onesb", [128, 128], BF16)
        onesf128 = sb("onesf128", [128, 128])
        onesf = sb("onesf", [128, 2])
        gn = sb("gn", [128, 16])
        gple = sb("gple", [128, 16])
        gsub = sb("gsub", [128, 1])
        gsubrow = sb("gsubrow", [1, 128])
        wdw = sb("wdw", [128, 8, 31])
        vecs = sb("vecs", [128, 3, 8])
        lamin = sb("lamin", [128, 4, 64])
        lamt = sb("lamt", [128, 2, 64])
        lam = sb("lam", [128, 4])
        cbias = sb("cbias", [128, 1])
        sel2t = sb("sel2t", [2, 2])
        coef = sb("coef", [2, 1])
        cosT = sb("cosT", [128, 17, 8])
        sinT = sb("sinT", [128, 17, 8])
        uT = sb("uT", [128, 16, 32 + NTOK], BF16)
        oT = sb("oT", [128, 8, NTOK], BF16)
        xa = [sb("xa0", [128, D])]
        xs = [sb("xs0", [128, D], BF16)]
        ss = sb("ss", [128, 4])
        rstd = sb("rstd", [128, 4])
        epsb = sb("epsb", [128, 1])
        wb = [sb(f"wb{i}", [128, 16, 512], BF16) for i in range(2)]
        qf = [sb(f"qf{i}", [128, 512]) for i in range(3)]
        qb = [sb(f"qb{i}", [128, 512], BF16) for i in range(3)]
        rt = [sb(f"rt{i}", [128, 64]) for i in range(4)]
        ps = [es.enter_context(nc.psum_tensor(f"ps{i}", [128, 512], F32)) for i in range(8)]
        psn = [0]

        def next_ps():
            i = psn[0] % 8
            psn[0] += 1
            return i

        def PK(i):
            return ("ps", i)

        S.op("dve", lambda e: e.memset(epsb[:], EPS), writes=["epsb"])
        S.op("dve", lambda e: e.memset(onesb[:], 1.0), writes=["onesb"])
        S.op("dve", lambda e: e.memset(onesf128[:], 1.0), writes=["onesf128"])
        S.op("dve", lambda e: e.memset(onesf[:], 1.0), writes=["onesf"])
        for (t_, a_, k_) in ((identb, ident_b, "identb"), (identf, ident_f, "identf"), (trib, tri_b, "trib"),
                             (gn, g_norm_pc, "gn"), (gple, g_ple_pc, "gple"), (gsub, g_subln_pc, "gsub"),
                             (gsubrow, g_subln_row, "gsubrow"), (wdw, wdw_pc, "wdw"), (vecs, vec_pc, "vecs"),
                             (lamin, lam4, "lamin"), (cbias, ctx_bias, "cbias"), (sel2t, sel2, "sel2"),
                             (cosT, cos_t, "cs"), (sinT, sin_t, "cs")):
            S.dma("sp", lambda e, t_=t_, a_=a_: e.dma_start(out=t_[:], in_=a_), writes=[k_])
        S.op("dve", lambda e: e.tensor_tensor(out=lamt[:, 0, :], in0=lamin[:, 0, :], in1=lamin[:, 1, :], op=ALU.mult),
             reads=["lamin"], writes=["lamt"])
        S.op("dve", lambda e: e.tensor_tensor(out=lamt[:, 1, :], in0=lamin[:, 2, :], in1=lamin[:, 3, :], op=ALU.mult),
             reads=["lamin"], writes=["lamt"])
        S.op("dve", lambda e: e.tensor_reduce(out=lam[:, 0:2], in_=lamt[:], axis=AX.X, op=ALU.add),
             reads=["lamt"], writes=["lam"])
        S.op("act", lambda e: e.activation(out=lam[:, 0:2], in_=lam[:, 0:2], func=AF.Exp), reads=["lam"], writes=["lam"])
        S.op("dve", lambda e: e.tensor_tensor(out=lam[:, 2:3], in0=lam[:, 0:1], in1=lam[:, 1:2], op=ALU.subtract),
             reads=["lam"], writes=["lam"])
        S.op("dve", lambda e: e.tensor_scalar(out=lam[:, 2:3], in0=lam[:, 2:3], scalar1=LAM_INIT, scalar2=None,
                                              op0=ALU.add), reads=["lam"], writes=["lam"])
        S.op("dve", lambda e: e.tensor_scalar(out=lam[:, 3:4], in0=lam[:, 2:3], scalar1=-1.0, scalar2=None,
                                              op0=ALU.mult), reads=["lam"], writes=["lam"])
        S.op("dve", lambda e: e.scalar_tensor_tensor(out=coef[:], in0=sel2t[:, 1:2], scalar=lam[0:2, 2:3],
                                                     in1=sel2t[:, 0:1], op0=ALU.mult, op1=ALU.add),
             reads=["lam", "sel2"], writes=["coef"])

        wslab_n = [0]

        def load_wslab(src_ap, col0, ncols, nchunk=16, view=None):
            i = wslab_n[0] % 2
            wslab_n[0] += 1
            buf = wb[i][:].rearrange("p c n -> p (c n)")[:, 0:nchunk * ncols].rearrange("p (c n) -> p c n", n=ncols)
            src = src_ap[:, col0:col0 + ncols].rearrange("(c p) n -> p c n", p=128)
            step = max(1, min(nchunk, 4096 // ncols))
            for c0 in range(0, nchunk, step):
                c1 = min(nchunk, c0 + step)
                S.dma("pool", lambda e, c0=c0, c1=c1: e.dma_start(out=buf[:, c0:c1, :], in_=src[:, c0:c1, :]),
                      writes=[("wb", i)])
            return i, buf

        def phase_a(x_rows_ap, n, dstT, gvec, ucol0, extra_last32=False, keep_x=None):
            i = 0
            if keep_x is None:
                S.dma("sp", lambda e: e.dma_start(out=xa[i][0:n, :], in_=x_rows_ap), writes=[("xa", i)])
                xin, xkey = xa[i][0:n, :], ("xa", i)
            else:
                xin, xkey = keep_x
            S.op("act", lambda e: e.activation(out=xs[i][0:n, :], in_=xin, func=AF.Square,
                                               accum_out=ss[0:n, i:i + 1]),
                 reads=[xkey], writes=[("xs", i), ("ss", i)])
            S.op("act", lambda e: e.activation(out=rstd[0:n, i:i + 1], in_=ss[0:n, i:i + 1], func=AF.Sqrt,
                                               scale=1.0 / D, bias=epsb[0:n, 0:1]),
                 reads=[("ss", i), "epsb"], writes=[("rstd", i)])
            S.op("dve", lambda e: e.reciprocal(out=rstd[0:n, i:i + 1], in_=rstd[0:n, i:i + 1]),
                 reads=[("rstd", i)], writes=[("rstd", i)])
            S.op("act", lambda e: e.activation(out=xs[i][0:n, :], in_=xin, func=AF.Copy,
                                               scale=rstd[0:n, i:i + 1]),
                 reads=[xkey, ("rstd", i)], writes=[("xs", i)])
            for g in range(2):
                pi = next_ps()
                pst = ps[pi][:].bitcast(BF16).rearrange("p (c t) -> p c t", c=8)
                fns = []
                for c8 in range(8):
                    c = g * 8 + c8
                    fns.append(lambda e, c=c, c8=c8: e.transpose(pst[:, c8, 0:n], xs[i][0:n, c * 128:(c + 1) * 128],
                                                                 identb[0:n, 0:n]))
                S.group("pe", fns, reads=[("xs", i), "identb"], writes=[PK(pi)])
                gbc = gvec[:, g * 8:(g + 1) * 8].unsqueeze(2).to_broadcast([128, 8, n])
                S.op("dve", lambda e: e.tensor_tensor(out=dstT[:, g * 8:(g + 1) * 8, ucol0:ucol0 + n],
                                                      in0=pst[:, :, 0:n], in1=gbc, op=ALU.mult),
                     reads=["gn", "gple"], writes=[PK(pi), ("uT", ucol0)])
                if extra_last32:
                    S.op("dve", lambda e: e.tensor_tensor(out=dstT[:, g * 8:(g + 1) * 8, 0:32],
                                                          in0=pst[:, :, 96:128],
                                                          in1=gvec[:, g * 8:(g + 1) * 8].unsqueeze(2).to_broadcast([128, 8, 32]),
                                                          op=ALU.mult),
                         reads=["gn"], writes=[PK(pi), ("uT", "l32")])

        qn = [0]

        def rope(dst, n, tt, ng):
            dv = dst[0:n, 0:ng * 64].rearrange("p (g d) -> p g d", d=64)
            cb = cosT[0:n, tt, :].unsqueeze(1).to_broadcast([n, ng, 8])
            sbb = sinT[0:n, tt, :].unsqueeze(1).to_broadcast([n, ng, 8])
            r = [rt[j][0:n, 0:ng * 8].rearrange("p (g d) -> p g d", d=8) for j in range(4)]
            dk = ("dst", id(dst))
            S.op("dve", lambda e: e.tensor_tensor(out=r[0], in0=dv[:, :, 0:8], in1=cb, op=ALU.mult),
                 reads=[dk, "cs"], writes=["rt0"])
            S.op("dve", lambda e: e.tensor_tensor(out=r[1], in0=dv[:, :, 8:16], in1=sbb, op=ALU.mult),
                 reads=[dk, "cs"], writes=["rt1"])
            S.op("dve", lambda e: e.tensor_tensor(out=r[2], in0=dv[:, :, 8:16], in1=cb, op=ALU.mult),
                 reads=[dk, "cs"], writes=["rt2"])
            S.op("dve", lambda e: e.tensor_tensor(out=r[3], in0=dv[:, :, 0:8], in1=sbb, op=ALU.mult),
                 reads=[dk, "cs"], writes=["rt3"])
            S.op("dve", lambda e: e.tensor_tensor(out=dv[:, :, 0:8], in0=r[0], in1=r[1], op=ALU.subtract),
                 reads=["rt0", "rt1"], writes=[dk])
            S.op("dve", lambda e: e.tensor_tensor(out=dv[:, :, 8:16], in0=r[2], in1=r[3], op=ALU.add),
                 reads=["rt2", "rt3"], writes=[dk])

        def mm_tok(wbuf, wi, n, lhs_fn, nk=16, ncols=512, extra=()):
            pi = next_ps()
            fns = []
            for c in range(nk):
                fns.append(lambda e, c=c: e.matmul(ps[pi][0:n, 0:ncols], lhs_fn(c), wbuf[:, c, :],
                                                   start=(c == 0), stop=(c == nk - 1)))
            S.group("pe", fns, reads=[("wb", wi)] + list(extra), writes=[PK(pi)])
            return pi

        def to_featmajor(src_bf, n, dstT, h0, key0, nh=4):
            pi = next_ps()
            pst = ps[pi][:].bitcast(BF16).rearrange("p (c t) -> p c t", c=8)
            fns = []
            for j in range(nh):
                fns.append(lambda e, j=j: e.transpose(pst[:, j, 0:n], src_bf[0:n, j * 128:(j + 1) * 128],
                                                      identb[0:n, 0:n]))
            S.group("pe", fns, reads=[("dstb", id(src_bf)), "identb"], writes=[PK(pi)])
            S.op("act", lambda e: e.activation(out=dstT[:, h0:h0 + nh, key0:key0 + n], in_=pst[:, 0:nh, 0:n],
                                               func=AF.Copy),
                 writes=[PK(pi), ("T", id(dstT), h0, key0)])

        def qkv_pass(tiles, do_q, kT, qT, vb):
            chunks = ([("q", 0), ("q", 1)] if do_q else []) + [("k", 0), ("k", 1), ("v", 0), ("v", 1)]
            col_of = {"q": 0, "k": 1024, "v": 2048}
            pending = []

            def flush(keep):
                while len(pending) > keep:
                    pending.pop(0)()

            for kind, hf in chunks:
                wi, wbuf = load_wslab(w_in, col_of[kind] + hf * 512, 512)
                for (n, ucol0, tt, key0, orow) in tiles:
                    pi = mm_tok(wbuf, wi, n, lambda c: uT[:, c, ucol0:ucol0 + n], extra=[("uT", ucol0)])
                    flush(0)
                    j = qn[0] % 3
                    qn[0] += 1
                    dk = ("dst", id(qf[j]))
                    S.op("act", lambda e: e.activation(out=qf[j][0:n, :], in_=ps[pi][0:n, :], func=AF.Copy),
                         writes=[PK(pi), dk])
                    if kind in ("q", "k"):
                        rope(qf[j], n, tt, 8)
                        S.op("act", lambda e: e.activation(out=qb[j][0:n, :], in_=qf[j][0:n, :], func=AF.Copy),
                             reads=[dk], writes=[("dstb", id(qb[j]))])
                        if kind == "k":
                            if orow is not None:
                                dst = (o_ks if orow == "s" else o_k)
                                r0 = 0 if orow == "s" else orow
                                S.dma("sp", lambda e: e.dma_start(out=dst[r0:r0 + n, hf * 512:(hf + 1) * 512],
                                                                  in_=qf[j][0:n, :]), reads=[dk],
                                      writes=(["o_ks"] if orow == "s" else []))
                            if key0 is not None:
                                pending.append(lambda j=j, n=n, hf=hf, key0=key0: to_featmajor(qb[j], n, kT, hf * 4, key0))
                        elif key0 is not None:
                            pending.append(lambda j=j, n=n, hf=hf, key0=key0: to_featmajor(qb[j], n, qT, hf * 4, key0 - S_OWN))
                        elif orow == "s":
                            S.dma("sp", lambda e: e.dma_start(out=scr_q.ap()[0:1, hf * 512:(hf + 1) * 512],
                                                              in_=qf[j][0:1, :]), reads=[dk], writes=["scr_q"])
                    else:
                        if key0 is not None:
                            S.op("dve", lambda e: e.tensor_copy(out=vb[0:n, key0 // 128, hf * 512:(hf + 1) * 512],
                                                                in_=qf[j][0:n, :]),
                                 reads=[dk], writes=[("vb", key0, hf)])
                        if orow is not None:
                            dst = (o_vs if orow == "s" else o_v)
                            r0 = 0 if orow == "s" else orow
                            S.dma("sp", lambda e: e.dma_start(out=dst[r0:r0 + n, hf * 512:(hf + 1) * 512],
                                                              in_=qf[j][0:n, :]), reads=[dk],
                                  writes=(["o_vs"] if orow == "s" else []))
            flush(0)

        with ExitStack() as s1:
            kT = sbx(s1, "kT", [128, 8, 2 * S_OWN], BF16)
            qT = sbx(s1, "qT", [128, 8, S_OWN], BF16)
            vb = sbx(s1, "vb", [128, 16, 1024], BF16)
            pT = [sbx(s1, f"pT{i}", [128, 512], BF16) for i in range(4)]
            of = [sbx(s1, f"of{i}", [128, 512]) for i in range(4)]
            sqe = sbx(s1, "sqe", [128, 512], BF16)

            for t in range(NT):
                phase_a(x_ctx[t * 128:(t + 1) * 128, :], 128, uT, gn, 32 + t * 128, extra_last32=(t == NT - 1))
            qkv_pass([(128, 32 + t * 128, t, t * 128, None) for t in range(NT)], False, kT, qT, vb)
            for t in range(NT):
                phase_a(x_own[t * 128:(t + 1) * 128, :], 128, uT, gn, 32 + t * 128)
            phase_a(x_s[:, :], NS, uT, gn, 32 + S_OWN)
            tiles = [(128, 32 + t * 128, 8 + t, S_OWN + t * 128, t * 128) for t in range(NT)]
            tiles.append((NS, 32 + S_OWN, 16, None, "s"))
            qkv_pass(tiles, True, kT, qT, vb)

            if stage >= 4:
                LA = 2
                its = []
                for h in range(8):
                    for j in range(2):
                        nkb = 8 + 4 * j + 4
                        for kb in range(nkb):
                            own_i = kb - 8
                            off, diag = 0, False
                            if own_i >= 4 * j:
                                off, diag = (own_i - 4 * j) * 128, True
                            for m in range(2):
                                its.append((h, j, kb, m, off, diag, nkb, h * 2 + j))

                def stage_a(ix):
                    h, j, kb, m, off, diag, nkb, gi = its[ix]
                    sb_i = ix % 4
                    N = 512 - off
                    q0 = j * 512 + off
                    S.group("pe", [lambda e: e.matmul(
                        ps[sb_i][:, 0:N], kT[m * 64:(m + 1) * 64, h, kb * 128:(kb + 1) * 128],
                        qT[m * 64:(m + 1) * 64, h, q0:q0 + N], start=True, stop=True)],
                        writes=[PK(sb_i)])
                    bias = cbias[:, 0:1] if kb < 8 else 0.0
                    S.op("act", lambda e: e.activation(
                        out=pT[sb_i][:, 0:N], in_=ps[sb_i][:, 0:N], func=AF.Exp, scale=0.125, bias=bias),
                        reads=["cbias"], writes=[PK(sb_i), ("pT", sb_i)])
                    if diag:
                        S.op("pool", lambda e: e.tensor_tensor(
                            out=pT[sb_i][:, 0:128], in0=pT[sb_i][:, 0:128], in1=trib[:], op=ALU.mult),
                            reads=["trib"], writes=[("pT", sb_i)])

                def lacc_of(gi, m):
                    c0 = ((gi % 2) * 2 + m) * 512
                    return xa[0][:, c0:c0 + 512]

                def obank(gi, m):
                    return 4 + 2 * (gi % 2) + m

                def stage_b(ix):
                    h, j, kb, m, off, diag, nkb, gi = its[ix]
                    sb_i = ix % 4
                    N = 512 - off
                    ob = obank(gi, m)
                    la = lacc_of(gi, m)
                    lk = ("lacc", gi % 2, m)
                    S.group("pe", [
                        lambda e: e.matmul(ps[ob][:, off:512], vb[:, kb, h * 128:(h + 1) * 128], pT[sb_i][:, 0:N],
                                           start=(kb == 0), stop=(kb == nkb - 1))],
                        reads=[("pT", sb_i)], writes=[PK(ob)])
                    ae = "pool" if m == 0 else "dve"
                    if kb == 0:
                        S.op(ae, lambda e: e.tensor_copy(out=la, in_=pT[sb_i][:, 0:512]),
                             reads=[("pT", sb_i)], writes=[lk])
                    else:
                        S.op(ae, lambda e: e.tensor_tensor(out=la[:, off:512], in0=la[:, off:512],
                                                           in1=pT[sb_i][:, 0:N], op=ALU.add),
                             reads=[("pT", sb_i)], writes=[lk])
                    if kb == nkb - 1 and m == 1:
                        epilogue(h, j, gi)

                def epilogue(h, j, gi):
                    o0, o1 = obank(gi, 0), obank(gi, 1)
                    S.group("pe", [lambda e, m=m: e.matmul(ps[m][:], onesf128[:], lacc_of(gi, m), start=True, stop=True)
                                   for m in range(2)],
                            reads=[("lacc", gi % 2, 0), ("lacc", gi % 2, 1), "onesf128"], writes=[PK(0), PK(1)])
                    for m_ in range(2):
                        S.op("act", lambda e, m_=m_: e.activation(out=of[m_][:], in_=ps[m_][:], func=AF.Ln),
                             writes=[PK(m_), f"of{m_}"])
                        S.op("act", lambda e, m_=m_: e.activation(out=of[m_][:], in_=of[m_][:], func=AF.Exp, scale=-1.0),
                             writes=[f"of{m_}"])
                    S.op("dve", lambda e: e.tensor_tensor(out=of[0][:], in0=ps[o0][:], in1=of[0][:], op=ALU.mult),
                         writes=[PK(o0), "of0"])
                    S.op("dve", lambda e: e.tensor_tensor(out=of[1][:], in0=ps[o1][:], in1=of[1][:], op=ALU.mult),
                         writes=[PK(o1), "of1"])
                    S.op("dve", lambda e: e.scalar_tensor_tensor(out=of[2][:], in0=of[1][:], scalar=lam[:, 3:4],
                                                                 in1=of[0][:], op0=ALU.mult, op1=ALU.add),
                         reads=["of0", "of1", "lam"], writes=["of2"])
                    S.op("dve", lambda e: e.tensor_tensor(out=sqe[:], in0=of[2][:], in1=of[2][:], op=ALU.mult),
                         reads=["of2"], writes=["sqe"])
                    S.group("pe", [lambda e: e.matmul(ps[2][:], onesb[:], sqe[:], start=True, stop=True)],
                            reads=["sqe", "onesb"], writes=[PK(2)])
                    S.op("act", lambda e: e.activation(out=of[3][:], in_=ps[2][:], func=AF.Ln, scale=1.0 / 128,
                                                       bias=epsb[:, 0:1]), reads=["epsb"], writes=[PK(2), "of3"])
                    S.op("act", lambda e: e.activation(out=of[3][:], in_=of[3][:], func=AF.Exp, scale=-0.5), writes=["of3"])
                    S.op("dve", lambda e: e.tensor_tensor(out=of[2][:], in0=of[2][:], in1=of[3][:], op=ALU.mult),
                         reads=["of3"], writes=["of2"])
                    S.op("dve", lambda e: e.tensor_scalar(
                        out=oT[:, h, j * 512:(j + 1) * 512], in0=of[2][:], scalar1=gsub[:, 0:1],
                        scalar2=1.0 - LAM_INIT, op0=ALU.mult, op1=ALU.mult),
                        reads=["of2", "gsub"], writes=[("oT", h)])

                n_it = len(its)
                for p_ in range(n_it // 2 + 1):
                    if 2 * p_ < n_it:
                        stage_a(2 * p_)
                        stage_a(2 * p_ + 1)
                    if p_ >= 1:
                        stage_b(2 * (p_ - 1))
                        stage_b(2 * (p_ - 1) + 1)
            S.barrier()
        if True:
            if stage >= 5 and sub != 5:
                with ExitStack() as s1b:
                    kc = [sbx(s1b, f"kc{i}", [128, 4096]) for i in range(2)]
                    vc = [sbx(s1b, f"vc{i}", [128, 2048], BF16) for i in range(2)]
                    ptT = sbx(s1b, "ptT", [128, 1], I32)
                    idx32 = sbx(s1b, "idx32", [128, 32], I32)
                    idx64 = sbx(s1b, "idx64", [128, 64], I32)
                    qkvr = sbx(s1b, "qkvr", [1, 3072])
                    vrb = sbx(s1b, "vrb", [1, 1024], BF16)
                    onesrow = sbx(s1b, "onesrow", [1, 128])
                    zrow = sbx(s1b, "zrow", [1, 512], BF16)
                    qrep = sbx(s1b, "qrep", [128, 1024])
                    sS = sbx(s1b, "sS", [128, 2048])
                    pS = sbx(s1b, "pS", [128, 2048], BF16)
                    lsum = sbx(s1b, "lsum", [128, 16])
                    qk1 = sbx(s1b, "qk1", [1, 1024])
                    pself = sbx(s1b, "pself", [1, 16])
                    pselfb = sbx(s1b, "pselfb", [1, 16], BF16)
                    rL = sbx(s1b, "rL", [2, 8])
                    on = sbx(s1b, "on", [2, 1024])
                    osr = sbx(s1b, "osr", [1, 1024])
                    osq = sbx(s1b, "osq", [1, 1024])
                    ors = sbx(s1b, "ors", [1, 8])
                    osTb = sbx(s1b, "osTb", [8, 1024], BF16)

                    S.dma("sp", lambda e: e.dma_start(out=ptT[:], in_=pt_T), writes=["ptT"])
                    for cc in range(32):
                        S.op("dve", lambda e, cc=cc: e.tensor_scalar(out=idx32[:, cc:cc + 1], in0=ptT[:], scalar1=32.0,
                                                                     scalar2=float(cc), op0=ALU.mult, op1=ALU.add),
                             reads=["ptT"], writes=["idx32"])
                    for cc in range(64):
                        S.op("dve", lambda e, cc=cc: e.tensor_scalar(out=idx64[:, cc:cc + 1], in0=ptT[:], scalar1=64.0,
                                                                     scalar2=float(cc), op0=ALU.mult, op1=ALU.add),
                             reads=["ptT"], writes=["idx64"])
                    S.op("dve", lambda e: e.memset(onesrow[:], 1.0), writes=["onesrow"])
                    S.dma("sp", lambda e: e.dma_start(out=qkvr[:, 0:1024], in_=scr_q.ap()), reads=["scr_q"], writes=["qkvr"])
                    S.dma("sp", lambda e: e.dma_start(out=qkvr[:, 1024:2048], in_=o_ks[0:1, :]), reads=["o_ks"], writes=["qkvr"])
                    S.dma("sp", lambda e: e.dma_start(out=qkvr[:, 2048:3072], in_=o_vs[0:1, :]), reads=["o_vs"], writes=["qkvr"])
                    S.op("act", lambda e: e.activation(out=vrb[:], in_=qkvr[:, 2048:3072], func=AF.Copy),
                         reads=["qkvr"], writes=["vrb"])
                    S.op("dve", lambda e: e.tensor_tensor(out=qk1[:], in0=qkvr[:, 0:1024], in1=qkvr[:, 1024:2048],
                                                          op=ALU.mult), reads=["qkvr"], writes=["qk1"])
                    S.op("dve", lambda e: e.tensor_reduce(out=pself[:], in_=qk1[:].rearrange("p (g d) -> p g d", d=64),
                                                          axis=AX.X, op=ALU.add), reads=["qk1"], writes=["pself"])
                    S.op("act", lambda e: e.activation(out=pself[:], in_=pself[:], func=AF.Exp, scale=0.125),
                         writes=["pself"])
                    S.op("dve", lambda e: e.tensor_copy(out=pselfb[:], in_=pself[:]), reads=["pself"], writes=["pselfb"])
                    for g in range(2):
                        pi = next_ps()
                        S.group("pe", [lambda e, g=g, pi=pi: e.matmul(ps[pi][:, :], onesrow[0:1, :],
                                                                      qkvr[0:1, g * 512:(g + 1) * 512], start=True, stop=True)],
                                reads=["onesrow", "qkvr"], writes=[PK(pi)])
                        S.op("act", lambda e, g=g, pi=pi: e.activation(out=qrep[:, g * 512:(g + 1) * 512], in_=ps[pi][:, :],
                                                                       func=AF.Copy), writes=[PK(pi), "qrep"])
                    for cc in range(32):
                        i = cc % 2
                        S.dma("pool", lambda e, cc=cc, i=i: e.indirect_dma_start(
                            out=kc[i][:], out_offset=None, in_=cache_k4[:, :],
                            in_offset=bass.IndirectOffsetOnAxis(ap=idx32[:, cc:cc + 1], axis=0)),
                            reads=["idx32"], writes=[("kc", i)])
                        kv = kc[i][:].rearrange("p (t d) -> p t d", d=1024)
                        S.op("dve", lambda e, kv=kv: e.tensor_tensor(
                            out=kv, in0=kv, in1=qrep[:].unsqueeze(1).to_broadcast([128, 4, 1024]), op=ALU.mult),
                            reads=["qrep"], writes=[("kc", i)])
                        S.op("dve", lambda e, cc=cc, i=i: e.tensor_reduce(
                            out=sS[:, cc * 64:(cc + 1) * 64], in_=kc[i][:].rearrange("p (g d) -> p g d", d=64),
                            axis=AX.X, op=ALU.add), reads=[("kc", i)], writes=["sS"])
                    S.op("act", lambda e: e.activation(out=pS[:], in_=sS[:], func=AF.Exp, scale=0.125),
                         reads=["sS"], writes=["pS"])
                    S.op("dve", lambda e: e.tensor_reduce(
                        out=lsum[:], in_=pS[:].rearrange("p (t g) -> p g t", g=16), axis=AX.X, op=ALU.add),
                        reads=["pS"], writes=["lsum"])
                    pa, pb_, pl = next_ps(), next_ps(), next_ps()
                    S.op("dve", lambda e: e.memset(zrow[:], 0.0), writes=["zrow"])
                    S.group("pe", [lambda e, bank=bank: e.matmul(ps[bank][0:2, :], zrow[0:1, 0:2], zrow[0:1, 0:512],
                                                                 start=True, stop=False) for bank in (pa, pb_)],
                            reads=["zrow"], writes=[PK(pa), PK(pb_)])
                    for cc in range(64):
                        i = cc % 2
                        S.dma("pool", lambda e, cc=cc, i=i: e.indirect_dma_start(
                            out=vc[i][:], out_offset=None, in_=cache_v8[:, :],
                            in_offset=bass.IndirectOffsetOnAxis(ap=idx64[:, cc:cc + 1], axis=0)),
                            reads=["idx64"], writes=[("vc", i)])
                        fns = []
                        for tl in range(2):
                            t = cc * 2 + tl
                            for h in range(8):
                                bank = pa if h < 4 else pb_
                                fns.append(lambda e, t=t, tl=tl, h=h, i=i, bank=bank: e.matmul(
                                    ps[bank][0:2, (h % 4) * 128:(h % 4 + 1) * 128], pS[:, t * 16 + h * 2:t * 16 + h * 2 + 2],
                                    vc[i][:, tl * 1024 + h * 128:tl * 1024 + (h + 1) * 128], start=False, stop=False))
                        S.group("pe", fns, reads=[("vc", i), "pS"], writes=[PK(pa), PK(pb_)])
                    for h in range(8):
                        bank = pa if h < 4 else pb_
                        S.group("pe", [lambda e, h=h, bank=bank: e.matmul(
                            ps[bank][0:2, (h % 4) * 128:(h % 4 + 1) * 128], pselfb[0:1, h * 2:h * 2 + 2],
                            vrb[0:1, h * 128:(h + 1) * 128], start=False, stop=(h % 4 == 3))],
                            reads=["pselfb", "vrb"], writes=[PK(bank)])
                        S.group("pe", [
                            lambda e, h=h: e.matmul(ps[pl][0:2, 2 * h:2 * h + 2], lsum[:, h * 2:h * 2 + 2], onesf[:, 0:2],
                                                    start=True, stop=False),
                            lambda e, h=h: e.matmul(ps[pl][0:2, 2 * h:2 * h + 2], pself[0:1, h * 2:h * 2 + 2], onesf[0:1, 0:2],
                                                    start=False, stop=True)],
                            reads=["lsum", "pself", "onesf"], writes=[PK(pl)])
                    S.op("dve", lambda e: e.reciprocal(out=rL[:], in_=ps[pl][0:2, 0:16].rearrange("p (b t) -> p b t", t=2)[:, :, 0]),
                         writes=[PK(pl), "rL"])
                    for g, bank in ((0, pa), (1, pb_)):
                        S.op("dve", lambda e, g=g, bank=bank: e.tensor_tensor(
                            out=on[:, g * 512:(g + 1) * 512].rearrange("p (b d) -> p b d", d=128),
                            in0=ps[bank][0:2, :].rearrange("p (b d) -> p b d", d=128),
                            in1=rL[:, g * 4:(g + 1) * 4].unsqueeze(2).to_broadcast([2, 4, 128]), op=ALU.mult),
                            reads=["rL"], writes=[PK(bank), ("on", g)])
                    for g in range(2):
                        pi = next_ps()
                        S.group("pe", [lambda e, g=g, pi=pi: e.matmul(ps[pi][0:1, :], coef[:], on[:, g * 512:(g + 1) * 512],
                                                                      start=True, stop=True)],
                                reads=["coef", ("on", g)], writes=[PK(pi)])
                        S.op("act", lambda e, g=g, pi=pi: e.activation(out=osr[:, g * 512:(g + 1) * 512], in_=ps[pi][0:1, :],
                                                                       func=AF.Copy), writes=[PK(pi), "osr"])
                    S.op("dve", lambda e: e.tensor_tensor(out=osq[:], in0=osr[:], in1=osr[:], op=ALU.mult),
                         reads=["osr"], writes=["osq"])
                    S.op("dve", lambda e: e.tensor_reduce(out=ors[:], in_=osq[:].rearrange("p (b d) -> p b d", d=128),
                                                          axis=AX.X, op=ALU.add), reads=["osq"], writes=["ors"])
                    S.op("act", lambda e: e.activation(out=ors[:], in_=ors[:], func=AF.Sqrt, scale=1.0 / 128,
                                                       bias=epsb[0:1, 0:1]), reads=["epsb"], writes=["ors"])
                    S.op("dve", lambda e: e.reciprocal(out=ors[:], in_=ors[:]), writes=["ors"])
                    osr3 = osr[:].rearrange("p (b d) -> p b d", d=128)
                    S.op("dve", lambda e: e.tensor_tensor(out=osr3, in0=osr3,
                                                          in1=ors[:].unsqueeze(2).to_broadcast([1, 8, 128]), op=ALU.mult),
                         reads=["ors"], writes=["osr"])
                    S.op("dve", lambda e: e.tensor_tensor(out=osr3, in0=osr3,
                                                          in1=gsubrow[:].unsqueeze(1).to_broadcast([1, 8, 128]), op=ALU.mult),
                         reads=["gsubrow"], writes=["osr"])
                    S.op("dve", lambda e: e.memset(osTb[:], 0.0), writes=[("dstb", id(osTb))])
                    S.op("dve", lambda e: e.tensor_scalar(out=osTb[0:1, :], in0=osr[:], scalar1=1.0 - LAM_INIT, scalar2=None,
                                                          op0=ALU.mult), reads=["osr"], writes=[("dstb", id(osTb))])
                    to_featmajor(osTb, NS, oT, 0, S_OWN, nh=8)
            S.barrier()
        if sub == 5 or stage < 5:
            S.op("dve", lambda e: e.memset(oT[:, :, S_OWN:NTOK], 0.0), writes=[("tgt", id(oT), oc_) for oc_ in range(8)])
        ocvT = sb("ocvT", [128, 8, NTOK], BF16)
        if stage >= 6:
          with ExitStack() as s2:
            gT = sbx(s2, "gT", [128, 8, 32 + NTOK], BF16)
            cT = sbx(s2, "cT", [128, 8, NTOK], BF16)
            acc = [sbx(s2, f"acc{i}", [128, NTOK]) for i in range(2)]
            sqb = [sbx(s2, f"sqb{i}", [128, NTOK], BF16) for i in range(2)]
            mu = sbx(s2, "mu", [128, NTOK])
            var = sbx(s2, "var", [128, NTOK])
            histT = sbx(s2, "histT", [128, 8, 8, 31])
            sct = sbx(s2, "sct", [30, 1024])
            tmpf = [sbx(s2, f"tmpf{i}", [128, 512]) for i in range(2)]
            tmpb = [sbx(s2, f"tmpb{i}", [128, 512], BF16) for i in range(2)]
            cvo = sbx(s2, "cvo", [30, 1024])
            cso = sbx(s2, "cso", [8, 1024])
            csum = sbx(s2, "csum", [128, 8, 8])

            tn = [0]
            own_chunks = [(32, 512), (32 + 512, 512), (32 + 1024, NS)]

            def gate_mult(col0, target):
                for hf2 in range(2):
                    wi, wbuf = load_wslab(w_in, col0 + hf2 * 512, 512)
                    for c4 in range(4):
                        oc = hf2 * 4 + c4
                        for (u0, N) in own_chunks:
                            pi = next_ps()
                            S.group("pe", [lambda e, c=c, pi=pi, u0=u0, N=N, c4=c4, wbuf=wbuf: e.matmul(
                                ps[pi][:, 0:N], wbuf[:, c, c4 * 128:(c4 + 1) * 128], uT[:, c, u0:u0 + N],
                                start=(c == 0), stop=(c == 15)) for c in range(16)],
                                reads=[("wb", wi), ("uT", "all")], writes=[PK(pi)])
                            i = tn[0] % 2
                            tn[0] += 1
                            S.op("act", lambda e, pi=pi, i=i, N=N: e.activation(out=tmpb[i][:, 0:N], in_=ps[pi][:, 0:N],
                                                                               func=AF.Silu),
                                 writes=[PK(pi), ("tmpb", i)])
                            t0 = u0 - 32
                            S.op("dve", lambda e, i=i, N=N, oc=oc, t0=t0: e.tensor_tensor(
                                out=target[:, oc, t0:t0 + N], in0=target[:, oc, t0:t0 + N], in1=tmpb[i][:, 0:N],
                                op=ALU.mult), reads=[("tmpb", i)], writes=[("tgt", id(target), oc)])

            def gate_chunk(wbuf, wkey, wc0, target, oc):
                for (u0, N) in own_chunks:
                    pi = next_ps()
                    S.group("pe", [lambda e, c=c, pi=pi, u0=u0, N=N: e.matmul(
                        ps[pi][:, 0:N], wbuf[:, c, wc0:wc0 + 128], uT[:, c, u0:u0 + N],
                        start=(c == 0), stop=(c == 15)) for c in range(16)],
                        reads=[wkey, ("uT", "all")], writes=[PK(pi)])
                    i = tn[0] % 2
                    tn[0] += 1
                    S.op("act", lambda e, pi=pi, i=i, N=N: e.activation(out=tmpb[i][:, 0:N], in_=ps[pi][:, 0:N],
                                                                       func=AF.Silu),
                         writes=[PK(pi), ("tmpb", i)])
                    t0 = u0 - 32
                    S.op("pool", lambda e, i=i, N=N, t0=t0: e.tensor_tensor(
                        out=target[:, oc, t0:t0 + N], in0=target[:, oc, t0:t0 + N], in1=tmpb[i][:, 0:N],
                        op=ALU.mult), reads=[("tmpb", i)], writes=[("tgt", id(target), oc)])

            xa_w = xa[0][:].bitcast(BF16).rearrange("p (c n) -> p c n", n=256)

            def load_gate_a(col0):
                src = w_in[:, col0:col0 + 256].rearrange("(c p) n -> p c n", p=128)
                for c0 in (0, 8):
                    S.dma("pool", lambda e, c0=c0: e.dma_start(out=xa_w[:, c0:c0 + 8, :], in_=src[:, c0:c0 + 8, :]),
                          writes=[("xa", 0)])

            for b in range(NS):
                S.dma("sp", lambda e, b=b: e.dma_start(out=sct[:], in_=state_conv[b, :, :]), writes=["sct"])
                pi = next_ps()
                pv = ps[pi][:, 0:240].rearrange("p (c k) -> p c k", k=30)
                S.group("pe", [lambda e, oc=oc, pv=pv: e.transpose(pv[:, oc, :], sct[:, oc * 128:(oc + 1) * 128],
                                                                    identf[0:30, 0:30]) for oc in range(8)],
                        reads=["sct", "identf"], writes=[PK(pi)])
                S.op("act", lambda e, b=b, pv=pv: e.activation(out=histT[:, :, b, 0:30], in_=pv, func=AF.Copy),
                     writes=[PK(pi), "histT"])
            S.op("dve", lambda e: e.tensor_tensor(out=histT[:, :, :, 0:30], in0=histT[:, :, :, 0:30],
                                                  in1=wdw[:, :, 0:30].unsqueeze(2).to_broadcast([128, 8, 8, 30]), op=ALU.mult),
                 reads=["wdw"], writes=["histT"])
            S.op("dve", lambda e: e.tensor_reduce(out=csum[:], in_=histT[:, :, :, 0:30], axis=AX.X, op=ALU.add),
                 reads=["histT"], writes=["csum"])

            st_chunks = [(0, 512), (512, 512), (1024, NS)]

            def conv_taps(oc):
                a = acc[oc % 2]
                ak = ("acc", oc % 2)
                S.op("dve", lambda e: e.tensor_scalar(
                    out=a[:, 0:S_OWN], in0=gT[:, oc, 2:2 + S_OWN], scalar1=wdw[:, oc, 0:1], scalar2=vecs[:, 0, oc:oc + 1],
                    op0=ALU.mult, op1=ALU.add), reads=[("gT", oc), "wdw", "vecs"], writes=[ak])
                for k in range(1, 31):
                    S.op("dve", lambda e, k=k: e.scalar_tensor_tensor(
                        out=a[:, 0:S_OWN], in0=gT[:, oc, 2 + k:2 + k + S_OWN], scalar=wdw[:, oc, k:k + 1],
                        in1=a[:, 0:S_OWN], op0=ALU.mult, op1=ALU.add), reads=[("gT", oc), "wdw"], writes=[ak])
                S.op("dve", lambda e: e.scalar_tensor_tensor(
                    out=a[:, S_OWN:NTOK], in0=gT[:, oc, 32 + S_OWN:32 + S_OWN + NS], scalar=wdw[:, oc, 30:31],
                    in1=csum[:, oc, :], op0=ALU.mult, op1=ALU.add), reads=[("gT", oc), "csum", "wdw"], writes=[ak])
                S.op("dve", lambda e: e.tensor_scalar(
                    out=a[:, S_OWN:NTOK], in0=a[:, S_OWN:NTOK], scalar1=vecs[:, 0, oc:oc + 1], scalar2=None, op0=ALU.add),
                    reads=["vecs"], writes=[ak])
                S.op("act", lambda e: e.activation(out=cT[:, oc, :], in_=a[:, :], func=AF.Copy),
                     reads=[ak], writes=[("cT", oc)])

            glu_chunks = [(0, 512), (512, 512), (1024, 32 + NTOK - 1024)]
            for hf2 in range(2):
                wia, wba = load_wslab(w_in, 4096 + hf2 * 512, 512)
                wib, wbb = load_wslab(w_in, 5120 + hf2 * 512, 512)
                for c4 in range(4):
                    oc = hf2 * 4 + c4
                    for (u0, N) in glu_chunks:
                        pa_, pb2 = next_ps(), next_ps()
                        S.group("pe", [lambda e, c=c, u0=u0, N=N, c4=c4, pa_=pa_, wba=wba: e.matmul(
                            ps[pa_][:, 0:N], wba[:, c, c4 * 128:(c4 + 1) * 128], uT[:, c, u0:u0 + N],
                            start=(c == 0), stop=(c == 15)) for c in range(16)],
                            reads=[("wb", wia), ("uT", "all")], writes=[PK(pa_)])
                        S.group("pe", [lambda e, c=c, u0=u0, N=N, c4=c4, pb2=pb2, wbb=wbb: e.matmul(
                            ps[pb2][:, 0:N], wbb[:, c, c4 * 128:(c4 + 1) * 128], uT[:, c, u0:u0 + N],
                            start=(c == 0), stop=(c == 15)) for c in range(16)],
                            reads=[("wb", wib), ("uT", "all")], writes=[PK(pb2)])
                        i = tn[0] % 2
                        tn[0] += 1
                        S.op("act", lambda e, pb2=pb2, i=i, N=N: e.activation(out=tmpf[i][:, 0:N], in_=ps[pb2][:, 0:N],
                                                                             func=AF.Sigmoid),
                             writes=[PK(pb2), ("tmpf", i)])
                        S.op("act", lambda e, pa_=pa_, N=N, oc=oc, u0=u0: e.activation(
                            out=gT[:, oc, u0:u0 + N], in_=ps[pa_][:, 0:N], func=AF.Copy),
                            writes=[PK(pa_), ("gT", oc)])
                        S.op("pool", lambda e, i=i, N=N, oc=oc, u0=u0: e.tensor_tensor(
                            out=gT[:, oc, u0:u0 + N], in0=gT[:, oc, u0:u0 + N], in1=tmpf[i][:, 0:N], op=ALU.mult),
                            reads=[("tmpf", i)], writes=[("gT", oc)])
                    if c4 % 2 == 0:
                        load_gate_a(3072 + oc * 128)
                    gate_chunk(xa_w, ("xa", 0), (c4 % 2) * 128, oT, oc)
                    conv_taps(oc)

            for (c0, n, dst_sb, dst_dram, key) in ((32 + S_OWN - 30, 30, cvo, o_conv[:, :], "cvo"),
                                                   (32 + S_OWN, NS, cso, None, "cso")):
                pi = next_ps()
                pst = ps[pi][:].bitcast(BF16).rearrange("p (c t) -> p c t", c=8)
                S.group("pe", [lambda e, oc=oc, c0=c0, n=n, pst=pst: e.transpose(pst[0:n, oc, :], gT[:, oc, c0:c0 + n], identb[:])
                               for oc in range(8)], reads=[("gT", oc) for oc in range(8)] + ["identb"], writes=[PK(pi)])
                S.op("act", lambda e, n=n, dst_sb=dst_sb, pst=pst: e.activation(
                    out=dst_sb[0:n, :].rearrange("p (c t) -> p c t", c=8), in_=pst[0:n, :, :], func=AF.Copy),
                    writes=[PK(pi), key])
                if dst_dram is not None:
                    S.dma("sp", lambda e, dst_dram=dst_dram, dst_sb=dst_sb, n=n: e.dma_start(out=dst_dram, in_=dst_sb[0:n, :]),
                          reads=[key])
            S.dma("sp", lambda e: e.dma_start(out=o_convs[:, 29, :], in_=cso[0:NS, :]), reads=["cso"])
            S.dma("sp", lambda e: e.dma_start(out=o_convs[:, 0:29, :], in_=state_conv[:, 1:30, :]))

            ps_s1 = [next_ps(), next_ps(), next_ps()]
            ps_s2 = [next_ps(), next_ps(), next_ps()]
            for oc in range(8):
                i = oc % 2
                S.op("act", lambda e, oc=oc, i=i: e.activation(out=sqb[i][:, :], in_=cT[:, oc, :], func=AF.Square),
                     reads=[("cT", oc)], writes=[("sqb", i)])
                for ci, (t0, N) in enumerate(st_chunks):
                    S.group("pe", [
                        lambda e, ci=ci, t0=t0, N=N, oc=oc: e.matmul(ps[ps_s1[ci]][:, 0:N], onesb[:], cT[:, oc, t0:t0 + N],
                                                                     start=(oc == 0), stop=(oc == 7)),
                        lambda e, ci=ci, t0=t0, N=N, oc=oc, i=i: e.matmul(ps[ps_s2[ci]][:, 0:N], onesb[:], sqb[i][:, t0:t0 + N],
                                                                          start=(oc == 0), stop=(oc == 7))],
                        reads=[("cT", oc), ("sqb", i), "onesb"], writes=[PK(ps_s1[ci]), PK(ps_s2[ci])])
            for ci, (t0, N) in enumerate(st_chunks):
                S.op("act", lambda e, ci=ci, t0=t0, N=N: e.activation(out=mu[:, t0:t0 + N], in_=ps[ps_s1[ci]][:, 0:N],
                                                                     func=AF.Copy, scale=1.0 / 1024),
                     writes=[PK(ps_s1[ci]), "mu"])
                S.op("act", lambda e, ci=ci, t0=t0, N=N: e.activation(out=var[:, t0:t0 + N], in_=ps[ps_s2[ci]][:, 0:N],
                                                                     func=AF.Copy, scale=1.0 / 1024),
                     writes=[PK(ps_s2[ci]), "var"])
            S.op("dve", lambda e: e.tensor_tensor(out=acc[0][:], in0=mu[:], in1=mu[:], op=ALU.mult),
                 reads=["mu"], writes=[("acc", 0)])
            S.op("dve", lambda e: e.tensor_tensor(out=var[:], in0=var[:], in1=acc[0][:], op=ALU.subtract),
                 reads=[("acc", 0)], writes=["var"])
            S.op("act", lambda e: e.activation(out=var[:], in_=var[:], func=AF.Sqrt, bias=epsb[:, 0:1]),
                 reads=["epsb"], writes=["var"])
            S.op("dve", lambda e: e.reciprocal(out=var[:], in_=var[:]), writes=["var"])
            for oc in range(8):
                i = oc % 2
                S.op("dve", lambda e, oc=oc, i=i: e.tensor_tensor(out=acc[i][:], in0=cT[:, oc, :], in1=mu[:], op=ALU.subtract),
                     reads=[("cT", oc), "mu"], writes=[("acc", i)])
                S.op("dve", lambda e, i=i: e.tensor_tensor(out=acc[i][:], in0=acc[i][:], in1=var[:], op=ALU.mult),
                     reads=["var"], writes=[("acc", i)])
                S.op("act", lambda e, oc=oc, i=i: e.activation(out=cT[:, oc, :], in_=acc[i][:], func=AF.Silu,
                                                               scale=vecs[:, 1, oc:oc + 1], bias=vecs[:, 2, oc:oc + 1]),
                     reads=[("acc", i), "vecs"], writes=[("cT", oc)])
            wi, wbuf = load_wslab(w_pw, 0, 1024, nchunk=8)
            for oc in range(8):
                for (t0, N) in st_chunks:
                    pi = next_ps()
                    S.group("pe", [lambda e, ic=ic, oc=oc, t0=t0, N=N, pi=pi: e.matmul(
                        ps[pi][:, 0:N], wbuf[:, ic, oc * 128:(oc + 1) * 128], cT[:, ic, t0:t0 + N],
                        start=(ic == 0), stop=(ic == 7)) for ic in range(8)],
                        reads=[("wb", wi)] + [("cT", ic) for ic in range(8)], writes=[PK(pi)])
                    S.op("act", lambda e, oc=oc, t0=t0, N=N, pi=pi: e.activation(out=ocvT[:, oc, t0:t0 + N], in_=ps[pi][:, 0:N],
                                                                                func=AF.Copy),
                         writes=[PK(pi), ("tgt", id(ocvT), oc)])
            gate_mult(6144, ocvT)
            S.barrier()

        if stage >= 7:
          with ExitStack() as s3:
            hb = sbx(s3, "hb", [128, 9, D])
            gfin = wb[0][:].rearrange("p c n -> p (c n)")[:, 0:4096].bitcast(F32)
            pT2 = sbx(s3, "ppT2", [128, 2, NTOK], BF16)
            pin = sbx(s3, "pin", [128, 256])
            pinb = sbx(s3, "pinb", [128, 256], BF16)
            sg = [sbx(s3, f"sg{i}", [128, 512]) for i in range(2)]
            tl = [(128, t * 128, x_own[t * 128:(t + 1) * 128, :], p_own[t * 128:(t + 1) * 128, :], o_y[t * 128:(t + 1) * 128, :])
                  for t in range(NT)]
            tl.append((NS, S_OWN, x_s[:, :], p_s[:, :], o_ys[:, :]))
            for ti, (n, t0, xap, pap, yap) in enumerate(tl):
                S.dma("sp", lambda e, ti=ti, n=n, xap=xap: e.dma_start(out=hb[0:n, ti, :], in_=xap), writes=[("hb", ti)])
            for sl in range(4):
                wi, wbuf = load_wslab(w_out, sl * 512, 512)
                for ti, (n, t0, xap, pap, yap) in enumerate(tl):
                    pi = mm_tok(wbuf, wi, n, lambda c, t0=t0, n=n: (oT[:, c, t0:t0 + n] if c < 8 else ocvT[:, c - 8, t0:t0 + n]))
                    S.op("dve", lambda e, ti=ti, n=n, sl=sl, pi=pi: e.tensor_tensor(
                        out=hb[0:n, ti, sl * 512:(sl + 1) * 512], in0=ps[pi][0:n, :], in1=hb[0:n, ti, sl * 512:(sl + 1) * 512],
                        op=ALU.add), writes=[PK(pi), ("hb", ti)])
            for ti, (n, t0, xap, pap, yap) in enumerate(tl):
                phase_a(None, n, uT, gple, 32 + t0, keep_x=(hb[0:n, ti, :], ("hb", ti)))
                S.dma("sp", lambda e, n=n, pap=pap: e.dma_start(out=pin[0:n, :], in_=pap), writes=["pin"])
                S.op("act", lambda e, n=n: e.activation(out=pinb[0:n, :], in_=pin[0:n, :], func=AF.Copy), reads=["pin"],
                     writes=[("dstb", id(pinb))])
                to_featmajor(pinb, n, pT2, 0, t0, nh=2)
            for sl in range(4):
                wi, wbuf = load_wslab(w_pg, sl * 512, 512)
                wi2, wbuf2 = load_wslab(w_ple, sl * 512, 512, nchunk=2)
                for ti, (n, t0, xap, pap, yap) in enumerate(tl):
                    pi = mm_tok(wbuf, wi, n, lambda c, t0=t0, n=n: uT[:, c, 32 + t0:32 + t0 + n], extra=[("uT", 32 + t0)])
                    pi2 = mm_tok(wbuf2, wi2, n, lambda c, t0=t0, n=n: pT2[:, c, t0:t0 + n], nk=2,
                                 extra=[("T", id(pT2), 0, t0)])
                    i = (sl * 9 + ti) % 2
                    S.op("act", lambda e, n=n, pi=pi, i=i: e.activation(out=sg[i][0:n, :], in_=ps[pi][0:n, :], func=AF.Sigmoid),
                         writes=[PK(pi), ("sg", i)])
                    S.op("dve", lambda e, n=n, pi2=pi2, i=i: e.tensor_tensor(out=sg[i][0:n, :], in0=ps[pi2][0:n, :],
                                                                            in1=sg[i][0:n, :], op=ALU.mult),
                         writes=[PK(pi2), ("sg", i)])
                    S.op("dve", lambda e, ti=ti, n=n, sl=sl, i=i: e.tensor_tensor(
                        out=hb[0:n, ti, sl * 512:(sl + 1) * 512], in0=sg[i][0:n, :], in1=hb[0:n, ti, sl * 512:(sl + 1) * 512],
                        op=ALU.add), reads=[("sg", i)], writes=[("hb", ti)])
            S.dma("sp", lambda e: e.dma_start(out=gfin, in_=g_final_bc), writes=[("wb", 0)])
            for ti, (n, t0, xap, pap, yap) in enumerate(tl):
                hk = ("hb", ti)
                S.op("act", lambda e, ti=ti, n=n: e.activation(out=xs[0][0:n, :], in_=hb[0:n, ti, :], func=AF.Square,
                                                               accum_out=ss[0:n, 1:2]),
                     reads=[hk], writes=[("xs", 0), ("ss", 1)])
                S.op("act", lambda e, n=n: e.activation(out=rstd[0:n, 1:2], in_=ss[0:n, 1:2], func=AF.Sqrt, scale=1.0 / D,
                                                        bias=epsb[0:n, 0:1]), reads=[("ss", 1), "epsb"], writes=[("rstd", 1)])
                S.op("dve", lambda e, n=n: e.reciprocal(out=rstd[0:n, 1:2], in_=rstd[0:n, 1:2]), writes=[("rstd", 1)])
                S.op("act", lambda e, ti=ti, n=n: e.activation(out=xa[0][0:n, :], in_=hb[0:n, ti, :], func=AF.Copy,
                                                               scale=rstd[0:n, 1:2]),
                     reads=[hk, ("rstd", 1)], writes=[("xa", 0)])
                S.op("dve", lambda e, n=n: e.tensor_tensor(out=xa[0][0:n, :], in0=xa[0][0:n, :], in1=gfin[0:n, :], op=ALU.mult),
                     reads=[("wb", 0)], writes=[("xa", 0)])
                S.dma("sp", lambda e, n=n, yap=yap: e.dma_start(out=yap, in_=xa[0][0:n, :]), reads=[("xa", 0)])

        for t in S.all_tokens():
            S.wait("sp", t)
        print("ops:", S.cnt, S.dcnt, "waits:", S.nwaits)
    return nc


def _prep_inputs(inp):
    import ml_dtypes
    f = lambda k: np.asarray(inp[k], np.float32)
    xp = f("x_prompt")
    xsamp = f("x_sample").reshape(NS, D)
    pp = f("p_prompt")[0]
    psamp = f("p_sample")[0].reshape(NS, 256)
    w_in = np.ascontiguousarray(f("w_in")[0])
    ck = f("cache_k")[0]
    cv = f("cache_v")[0]
    pc16 = lambda v: np.ascontiguousarray(v.reshape(16, 128).T)
    pc8 = lambda v: np.ascontiguousarray(v.reshape(8, 128).T)
    ident = np.eye(128, dtype=np.float32)
    tri = (np.arange(128)[:, None] <= np.arange(128)[None, :]).astype(np.float32)
    common = {
        "w_in": w_in,
        "w_pw": np.ascontiguousarray(f("w_pw")[0]), "w_out": np.ascontiguousarray(f("w_out")[0]),
        "w_pg": np.ascontiguousarray(f("w_pg")[0]), "w_ple": np.ascontiguousarray(f("w_ple")[0]),
        "cache_k4": np.ascontiguousarray(ck).reshape(1280 * 32, 4096),
        "cache_v8": np.ascontiguousarray(cv).reshape(1280 * 64, 2048),
        "g_norm_pc": pc16(f("g_norm")[0]), "g_ple_pc": pc16(f("g_ple")[0]),
        "g_final_bc": np.ascontiguousarray(np.broadcast_to(f("g_final")[None, :], (128, D))),
        "g_subln_pc": np.ascontiguousarray(f("g_subln")[0].reshape(128, 1)),
        "g_subln_row": np.ascontiguousarray(f("g_subln")[0].reshape(1, 128)),
        "wdw_pc": np.ascontiguousarray(f("w_dw")[0].T.reshape(8, 128, 31).transpose(1, 0, 2)),
        "vec_pc": np.ascontiguousarray(np.stack([pc8(f("b_dw")[0]), pc8(f("g_cln")[0]), pc8(f("b_cln")[0])], axis=1)),
        "lam4": np.ascontiguousarray(np.broadcast_to(
            np.stack([f("lam_q1")[0], f("lam_k1")[0], f("lam_q2")[0], f("lam_k2")[0]])[None], (128, 4, 64))),
        "ident_b": ident.astype(ml_dtypes.bfloat16), "ident_f": ident, "tri_b": tri.astype(ml_dtypes.bfloat16),
        "sel2": np.array([[1.0, 0.0], [0.0, -1.0]], np.float32),
    }
    maps = []
    for c in range(8):
        b, hf = c // 2, c % 2
        x_own = np.ascontiguousarray(xp[b, hf * S_OWN:(hf + 1) * S_OWN])
        x_ctx = np.ascontiguousarray(xp[b, 0:S_OWN]) if hf == 1 else np.zeros((S_OWN, D), np.float32)
        pos = np.concatenate([np.arange(S_OWN) + (hf - 1) * S_OWN, np.arange(S_OWN) + hf * S_OWN,
                              np.full(128, PAST)]).astype(np.float32)
        cos, sin = _rope_tables(pos)
        m = dict(common)
        m.update({
            "x_ctx": x_ctx, "x_own": x_own,
            "p_own": np.ascontiguousarray(pp[b, hf * S_OWN:(hf + 1) * S_OWN]),
            "x_s": np.ascontiguousarray(np.roll(xsamp, -c, axis=0)),
            "p_s": np.ascontiguousarray(np.roll(psamp, -c, axis=0)),
            "state_conv": np.ascontiguousarray(np.roll(f("state_conv")[0], -c, axis=0)),
            "pt_T": np.ascontiguousarray(np.asarray(inp["page_table"], np.int32)[c].reshape(128, 1)),
            "cos_t": np.ascontiguousarray(cos.reshape(17, 128, 8).transpose(1, 0, 2)),
            "sin_t": np.ascontiguousarray(sin.reshape(17, 128, 8).transpose(1, 0, 2)),
            "ctx_bias": np.full((128, 1), 0.0 if hf == 1 else NEG, np.float32),
        })
        maps.append(m)
    return maps


def _assemble(results):
    y = np.zeros((4, 2048, D), np.float32)
    nk = np.zeros((1, 4, 2048, 8, 128), np.float32)
    nv = np.zeros((1, 4, 2048, 8, 128), np.float32)
    ncp = np.zeros((1, 4, 30, 1024), np.float32)
    for c in range(8):
        b, hf = c // 2, c % 2
        r = results[c]
        sl = slice(hf * S_OWN, (hf + 1) * S_OWN)
        y[b, sl] = r["o_y"]
        nk[0, b, sl] = r["o_k"].reshape(S_OWN, 8, 128)
        nv[0, b, sl] = r["o_v"].reshape(S_OWN, 8, 128)
        if hf == 1:
            ncp[0, b] = r["o_conv"]
    r0 = results[0]
    ys = np.stack([np.asarray(results[c]["o_ys"], np.float32)[0] for c in range(8)]).reshape(8, 1, D)
    nks = np.asarray(r0["o_ks"], np.float32).reshape(1, 8, 1, 8, 128)
    nvs = np.asarray(r0["o_vs"], np.float32).reshape(1, 8, 1, 8, 128)
    ncs = np.asarray(r0["o_convs"], np.float32).reshape(1, 8, 30, 1024)
    return (y, ys, nk, nv, ncp, nks, nvs, ncs)


_NC_CACHE = {}


def kernel(**inp):
    if "nc" not in _NC_CACHE:
        _NC_CACHE["nc"] = build()
    nc = _NC_CACHE["nc"]
    maps = _prep_inputs(inp)
    res = run_bass_kernel_spmd(nc, maps, core_ids=list(range(8)))
    return _assemble(res.results)
```

```python
import math
from contextlib import ExitStack

import numpy as np
import concourse.bass as bass
import concourse.mybir as mybir
from concourse.bass_utils import run_bass_kernel_spmd

F32 = mybir.dt.float32
BF16 = mybir.dt.bfloat16
I32 = mybir.dt.int32
ALU = mybir.AluOpType
AF = mybir.ActivationFunctionType
AX = mybir.AxisListType

D = 2048
S_OWN = 1024
NT = 8
NS = 8
DIN = 7168
EPS = 1e-6
LAM_INIT = 0.8 - 0.6 * math.exp(-0.3 * 0)
PAST = 16384
ROPE_THETA = 500000.0
NEG = -30000.0


class Tok:
    __slots__ = ("sem", "val", "eng")

    def __init__(self, sem, val, eng):
        self.sem, self.val, self.eng = sem, val, eng


class Sched:
    NQ = 8
    NQP = 4

    def __init__(self, nc):
        self.nc = nc
        self.eng = {"pe": nc.tensor, "act": nc.scalar, "dve": nc.vector, "pool": nc.gpsimd, "sp": nc.sync}
        self.sem = {e: nc.alloc_semaphore(f"sem_{e}") for e in ("pe", "act", "dve", "pool")}
        self.cnt = {e: 0 for e in self.sem}
        self.nq = {"sp": self.NQ, "pool": self.NQP, "act": 2}
        self.dsem = {q: [nc.alloc_semaphore(f"dsem_{q}_{i}") for i in range(self.nq[q])] for q in ("sp", "pool", "act")}
        self.dcnt = {q: 0 for q in self.dsem}
        self.waited = {}
        self.lastw = {}
        self.readers = {}
        self.nwaits = 0

    def wait(self, e, tok):
        if tok is None:
            return
        if tok.eng == "pe" and e == "pe":
            return
        k = (e, id(tok.sem))
        if self.waited.get(k, 0) >= tok.val:
            return
        self.waited[k] = tok.val
        self.eng[e].wait_ge(tok.sem, tok.val)
        self.nwaits += 1

    def _deps(self, e, reads, writes):
        deps = []
        for k in reads:
            t = self.lastw.get(k)
            if t is not None:
                deps.append(t)
        for k in writes:
            t = self.lastw.get(k)
            if t is not None:
                deps.append(t)
            deps.extend(self.readers.get(k, {}).values())
        deps.sort(key=lambda t: -t.val)
        for t in deps:
            self.wait(e, t)

    def _record(self, tok, reads, writes):
        for k in reads:
            d = self.readers.setdefault(k, {})
            d[id(tok.sem)] = tok
        for k in writes:
            self.lastw[k] = tok
            self.readers[k] = {}

    def op(self, e, fn, reads=(), writes=()):
        self._deps(e, reads, writes)
        ins = fn(self.eng[e])
        self.cnt[e] += 1
        ins.then_inc(self.sem[e], 1)
        tok = Tok(self.sem[e], self.cnt[e], e)
        self._record(tok, reads, writes)
        return tok

    def group(self, e, fns, reads=(), writes=()):
        self._deps(e, reads, writes)
        ins = None
        for fn in fns:
            ins = fn(self.eng[e])
        self.cnt[e] += 1
        ins.then_inc(self.sem[e], 1)
        tok = Tok(self.sem[e], self.cnt[e], e)
        self._record(tok, reads, writes)
        return tok

    def dma(self, q, fn, reads=(), writes=()):
        i = self.dcnt[q]
        nq = self.nq[q]
        slot, rnd = i % nq, i // nq
        sem = self.dsem[q][slot]
        if rnd > 0:
            self.wait(q, Tok(sem, 16 * rnd, "dma"))
        self._deps(q, reads, writes)
        ins = fn(self.eng[q])
        ins.then_inc(sem, 16)
        self.dcnt[q] += 1
        tok = Tok(sem, 16 * (rnd + 1), "dma")
        self._record(tok, reads, writes)
        return tok

    def all_tokens(self):
        toks = [Tok(self.sem[e], self.cnt[e], e) for e in self.sem if self.cnt[e] > 0]
        for q in self.dsem:
            n = self.dcnt[q]
            nq = self.nq[q]
            for s in range(nq):
                r = (n - s + nq - 1) // nq
                if r > 0:
                    toks.append(Tok(self.dsem[q][s], 16 * r, "dma"))
        return toks

    def barrier(self, engines=("pe", "act", "dve", "pool", "sp")):
        toks = self.all_tokens()
        for e in engines:
            for t in toks:
                if t.eng == e and e != "pe":
                    pass
                self.wait(e, t)
        self.lastw.clear()
        self.readers.clear()


def _rope_tables(pos):
    inv = ROPE_THETA ** (-np.arange(0, 16, 2, dtype=np.float32) / 16.0)
    ang = pos.astype(np.float32)[:, None] * inv[None, :].astype(np.float32)
    return np.cos(ang).astype(np.float32), np.sin(ang).astype(np.float32)


def build(stage=99, debug=False, sub=0):
    nc = bass.Bass("TRN2", target_bir_lowering=False)
    dt = nc.dram_tensor

    def din(name, shape, dtype=F32):
        return dt(name, list(shape), dtype, kind="ExternalInput").ap()

    def dout(name, shape, dtype=F32):
        return dt(name, list(shape), dtype, kind="ExternalOutput").ap()

    x_ctx = din("x_ctx", [S_OWN, D])
    x_own = din("x_own", [S_OWN, D])
    x_s = din("x_s", [NS, D])
    p_own = din("p_own", [S_OWN, 256])
    p_s = din("p_s", [NS, 256])
    w_in = din("w_in", [D, DIN])
    w_pw = din("w_pw", [1024, 1024])
    w_out = din("w_out", [D, D])
    w_pg = din("w_pg", [D, D])
    w_ple = din("w_ple", [256, D])
    cache_k4 = din("cache_k4", [1280 * 32, 4096])
    cache_v8 = din("cache_v8", [1280 * 64, 2048])
    pt_T = din("pt_T", [128, 1], I32)
    state_conv = din("state_conv", [NS, 30, 1024])
    g_norm_pc = din("g_norm_pc", [128, 16])
    g_ple_pc = din("g_ple_pc", [128, 16])
    g_final_bc = din("g_final_bc", [128, D])
    g_subln_pc = din("g_subln_pc", [128, 1])
    g_subln_row = din("g_subln_row", [1, 128])
    wdw_pc = din("wdw_pc", [128, 8, 31])
    vec_pc = din("vec_pc", [128, 3, 8])
    lam4 = din("lam4", [128, 4, 64])
    cos_t = din("cos_t", [128, 17, 8])
    sin_t = din("sin_t", [128, 17, 8])
    ident_b = din("ident_b", [128, 128], BF16)
    ident_f = din("ident_f", [128, 128])
    tri_b = din("tri_b", [128, 128], BF16)
    ctx_bias = din("ctx_bias", [128, 1])
    sel2 = din("sel2", [2, 2])

    o_y = dout("o_y", [S_OWN, D])
    o_ys = dout("o_ys", [NS, D])
    o_k = dout("o_k", [S_OWN, 1024])
    o_v = dout("o_v", [S_OWN, 1024])
    o_conv = dout("o_conv", [30, 1024])
    o_ks = dout("o_ks", [NS, 1024])
    o_vs = dout("o_vs", [NS, 1024])
    o_convs = dout("o_convs", [NS, 30, 1024])

    scr_q = dt("scr_q", [1, 1024], F32)

    NTOK = S_OWN + NS

    with ExitStack() as es:
        def sbx(stack, name, shape, dtype=F32):
            return stack.enter_context(nc.sbuf_tensor(name, list(shape), dtype))

        def sb(name, shape, dtype=F32):
            return sbx(es, name, shape, dtype)

        S = Sched(nc)

        identb = sb("identb", [128, 128], BF16)
        identf = sb("identf", [128, 128], F32)
        trib = sb("trib", [128, 128], BF16)
        onesb = sb("onesb", [128, 128], BF16)
        onesf128 = sb("onesf128", [128, 128])
        onesf = sb("onesf", [128, 2])
        gn = sb("gn", [128, 16])
        gple = sb("gple", [128, 16])
        gsub = sb("gsub", [128, 1])
        gsubrow = sb("gsubrow", [1, 128])
        wdw = sb("wdw", [128, 8, 31])
        vecs = sb("vecs", [128, 3, 8])
        lamin = sb("lamin", [128, 4, 64])
        lamt = sb("lamt", [128, 2, 64])
        lam = sb("lam", [128, 4])
        cbias = sb("cbias", [128, 1])
        sel2t = sb("sel2t", [2, 2])
        coef = sb("coef", [2, 1])
        cosT = sb("cosT", [128, 17, 8])
        sinT = sb("sinT", [128, 17, 8])
        uT = sb("uT", [128, 16, 32 + NTOK], BF16)
        oT = sb("oT", [128, 8, NTOK], BF16)
        xa = [sb("xa0", [128, D])]
        xs = [sb("xs0", [128, D], BF16)]
        ss = sb("ss", [128, 4])
        rstd = sb("rstd", [128, 4])
        epsb = sb("epsb", [128, 1])
        wb = [sb(f"wb{i}", [128, 16, 512], BF16) for i in range(2)]
        qf = [sb(f"qf{i}", [128, 512]) for i in range(3)]
        qb = [sb(f"qb{i}", [128, 512], BF16) for i in range(3)]
        rt = [sb(f"rt{i}", [128, 64]) for i in range(4)]
        ps = [es.enter_context(nc.psum_tensor(f"ps{i}", [128, 512], F32)) for i in range(8)]
        psn = [0]

        def next_ps():
            i = psn[0] % 8
            psn[0] += 1
            return i

        def PK(i):
            return ("ps", i)

        S.op("dve", lambda e: e.memset(epsb[:], EPS), writes=["epsb"])
        S.op("dve", lambda e: e.memset(onesb[:], 1.0), writes=["onesb"])
        S.op("dve", lambda e: e.memset(onesf128[:], 1.0), writes=["onesf128"])
        S.op("dve", lambda e: e.memset(onesf[:], 1.0), writes=["onesf"])
        for (t_, a_, k_) in ((identb, ident_b, "identb"), (identf, ident_f, "identf"), (trib, tri_b, "trib"),
                             (gn, g_norm_pc, "gn"), (gple, g_ple_pc, "gple"), (gsub, g_subln_pc, "gsub"),
                             (gsubrow, g_subln_row, "gsubrow"), (wdw, wdw_pc, "wdw"), (vecs, vec_pc, "vecs"),
                             (lamin, lam4, "lamin"), (cbias, ctx_bias, "cbias"), (sel2t, sel2, "sel2"),
                             (cosT, cos_t, "cs"), (sinT, sin_t, "cs")):
            S.dma("sp", lambda e, t_=t_, a_=a_: e.dma_start(out=t_[:], in_=a_), writes=[k_])
        S.op("dve", lambda e: e.tensor_tensor(out=lamt[:, 0, :], in0=lamin[:, 0, :], in1=lamin[:, 1, :], op=ALU.mult),
             reads=["lamin"], writes=["lamt"])
        S.op("dve", lambda e: e.tensor_tensor(out=lamt[:, 1, :], in0=lamin[:, 2, :], in1=lamin[:, 3, :], op=ALU.mult),
             reads=["lamin"], writes=["lamt"])
        S.op("dve", lambda e: e.tensor_reduce(out=lam[:, 0:2], in_=lamt[:], axis=AX.X, op=ALU.add),
             reads=["lamt"], writes=["lam"])
        S.op("act", lambda e: e.activation(out=lam[:, 0:2], in_=lam[:, 0:2], func=AF.Exp), reads=["lam"], writes=["lam"])
        S.op("dve", lambda e: e.tensor_tensor(out=lam[:, 2:3], in0=lam[:, 0:1], in1=lam[:, 1:2], op=ALU.subtract),
             reads=["lam"], writes=["lam"])
        S.op("dve", lambda e: e.tensor_scalar(out=lam[:, 2:3], in0=lam[:, 2:3], scalar1=LAM_INIT, scalar2=None,
                                              op0=ALU.add), reads=["lam"], writes=["lam"])
        S.op("dve", lambda e: e.tensor_scalar(out=lam[:, 3:4], in0=lam[:, 2:3], scalar1=-1.0, scalar2=None,
                                              op0=ALU.mult), reads=["lam"], writes=["lam"])
        S.op("dve", lambda e: e.scalar_tensor_tensor(out=coef[:], in0=sel2t[:, 1:2], scalar=lam[0:2, 2:3],
                                                     in1=sel2t[:, 0:1], op0=ALU.mult, op1=ALU.add),
             reads=["lam", "sel2"], writes=["coef"])

        wslab_n = [0]

        def load_wslab(src_ap, col0, ncols, nchunk=16, view=None):
            i = wslab_n[0] % 2
            wslab_n[0] += 1
            buf = wb[i][:].rearrange("p c n -> p (c n)")[:, 0:nchunk * ncols].rearrange("p (c n) -> p c n", n=ncols)
            src = src_ap[:, col0:col0 + ncols].rearrange("(c p) n -> p c n", p=128)
            step = max(1, min(nchunk, 4096 // ncols))
            for c0 in range(0, nchunk, step):
                c1 = min(nchunk, c0 + step)
                S.dma("pool", lambda e, c0=c0, c1=c1: e.dma_start(out=buf[:, c0:c1, :], in_=src[:, c0:c1, :]),
                      writes=[("wb", i)])
            return i, buf

        def phase_a(x_rows_ap, n, dstT, gvec, ucol0, extra_last32=False, keep_x=None):
            i = 0
            if keep_x is None:
                S.dma("sp", lambda e: e.dma_start(out=xa[i][0:n, :], in_=x_rows_ap), writes=[("xa", i)])
                xin, xkey = xa[i][0:n, :], ("xa", i)
            else:
                xin, xkey = keep_x
            S.op("act", lambda e: e.activation(out=xs[i][0:n, :], in_=xin, func=AF.Square,
                                               accum_out=ss[0:n, i:i + 1]),
                 reads=[xkey], writes=[("xs", i), ("ss", i)])
            S.op("act", lambda e: e.activation(out=rstd[0:n, i:i + 1], in_=ss[0:n, i:i + 1], func=AF.Sqrt,
                                               scale=1.0 / D, bias=epsb[0:n, 0:1]),
                 reads=[("ss", i), "epsb"], writes=[("rstd", i)])
            S.op("dve", lambda e: e.reciprocal(out=rstd[0:n, i:i + 1], in_=rstd[0:n, i:i + 1]),
                 reads=[("rstd", i)], writes=[("rstd", i)])
            S.op("act", lambda e: e.activation(out=xs[i][0:n, :], in_=xin, func=AF.Copy,
                                               scale=rstd[0:n, i:i + 1]),
                 reads=[xkey, ("rstd", i)], writes=[("xs", i)])
            for g in range(2):
                pi = next_ps()
                pst = ps[pi][:].bitcast(BF16).rearrange("p (c t) -> p c t", c=8)
                fns = []
                for c8 in range(8):
                    c = g * 8 + c8
                    fns.append(lambda e, c=c, c8=c8: e.transpose(pst[:, c8, 0:n], xs[i][0:n, c * 128:(c + 1) * 128],
                                                                 identb[0:n, 0:n]))
                S.group("pe", fns, reads=[("xs", i), "identb"], writes=[PK(pi)])
                gbc = gvec[:, g * 8:(g + 1) * 8].unsqueeze(2).to_broadcast([128, 8, n])
                S.op("dve", lambda e: e.tensor_tensor(out=dstT[:, g * 8:(g + 1) * 8, ucol0:ucol0 + n],
                                                      in0=pst[:, :, 0:n], in1=gbc, op=ALU.mult),
                     reads=["gn", "gple"], writes=[PK(pi), ("uT", ucol0)])
                if extra_last32:
                    S.op("dve", lambda e: e.tensor_tensor(out=dstT[:, g * 8:(g + 1) * 8, 0:32],
                                                          in0=pst[:, :, 96:128],
                                                          in1=gvec[:, g * 8:(g + 1) * 8].unsqueeze(2).to_broadcast([128, 8, 32]),
                                                          op=ALU.mult),
                         reads=["gn"], writes=[PK(pi), ("uT", "l32")])

        qn = [0]

        def rope(dst, n, tt, ng):
            dv = dst[0:n, 0:ng * 64].rearrange("p (g d) -> p g d", d=64)
            cb = cosT[0:n, tt, :].unsqueeze(1).to_broadcast([n, ng, 8])
            sbb = sinT[0:n, tt, :].unsqueeze(1).to_broadcast([n, ng, 8])
            r = [rt[j][0:n, 0:ng * 8].rearrange("p (g d) -> p g d", d=8) for j in range(4)]
            dk = ("dst", id(dst))
            S.op("dve", lambda e: e.tensor_tensor(out=r[0], in0=dv[:, :, 0:8], in1=cb, op=ALU.mult),
                 reads=[dk, "cs"], writes=["rt0"])
            S.op("dve", lambda e: e.tensor_tensor(out=r[1], in0=dv[:, :, 8:16], in1=sbb, op=ALU.mult),
                 reads=[dk, "cs"], writes=["rt1"])
            S.op("dve", lambda e: e.tensor_tensor(out=r[2], in0=dv[:, :, 8:16], in1=cb, op=ALU.mult),
                 reads=[dk, "cs"], writes=["rt2"])
            S.op("dve", lambda e: e.tensor_tensor(out=r[3], in0=dv[:, :, 0:8], in1=sbb, op=ALU.mult),
                 reads=[dk, "cs"], writes=["rt3"])
            S.op("dve", lambda e: e.tensor_tensor(out=dv[:, :, 0:8], in0=r[0], in1=r[1], op=ALU.subtract),
                 reads=["rt0", "rt1"], writes=[dk])
            S.op("dve", lambda e: e.tensor_tensor(out=dv[:, :, 8:16], in0=r[2], in1=r[3], op=ALU.add),
                 reads=["rt2", "rt3"], writes=[dk])

        def mm_tok(wbuf, wi, n, lhs_fn, nk=16, ncols=512, extra=()):
            pi = next_ps()
            fns = []
            for c in range(nk):
                fns.append(lambda e, c=c: e.matmul(ps[pi][0:n, 0:ncols], lhs_fn(c), wbuf[:, c, :],
                                                   start=(c == 0), stop=(c == nk - 1)))
            S.group("pe", fns, reads=[("wb", wi)] + list(extra), writes=[PK(pi)])
            return pi

        def to_featmajor(src_bf, n, dstT, h0, key0, nh=4):
            pi = next_ps()
            pst = ps[pi][:].bitcast(BF16).rearrange("p (c t) -> p c t", c=8)
            fns = []
            for j in range(nh):
                fns.append(lambda e, j=j: e.transpose(pst[:, j, 0:n], src_bf[0:n, j * 128:(j + 1) * 128],
                                                      identb[0:n, 0:n]))
            S.group("pe", fns, reads=[("dstb", id(src_bf)), "identb"], writes=[PK(pi)])
            S.op("act", lambda e: e.activation(out=dstT[:, h0:h0 + nh, key0:key0 + n], in_=pst[:, 0:nh, 0:n],
                                               func=AF.Copy),
                 writes=[PK(pi), ("T", id(dstT), h0, key0)])

        def qkv_pass(tiles, do_q, kT, qT, vb):
            chunks = ([("q", 0), ("q", 1)] if do_q else []) + [("k", 0), ("k", 1), ("v", 0), ("v", 1)]
            col_of = {"q": 0, "k": 1024, "v": 2048}
            pending = []

            def flush(keep):
                while len(pending) > keep:
                    pending.pop(0)()

            for kind, hf in chunks:
                wi, wbuf = load_wslab(w_in, col_of[kind] + hf * 512, 512)
                for (n, ucol0, tt, key0, orow) in tiles:
                    pi = mm_tok(wbuf, wi, n, lambda c: uT[:, c, ucol0:ucol0 + n], extra=[("uT", ucol0)])
                    flush(0)
                    j = qn[0] % 3
                    qn[0] += 1
                    dk = ("dst", id(qf[j]))
                    S.op("act", lambda e: e.activation(out=qf[j][0:n, :], in_=ps[pi][0:n, :], func=AF.Copy),
                         writes=[PK(pi), dk])
                    if kind in ("q", "k"):
                        rope(qf[j], n, tt, 8)
                        S.op("act", lambda e: e.activation(out=qb[j][0:n, :], in_=qf[j][0:n, :], func=AF.Copy),
                             reads=[dk], writes=[("dstb", id(qb[j]))])
                        if kind == "k":
                            if orow is not None:
                                dst = (o_ks if orow == "s" else o_k)
                                r0 = 0 if orow == "s" else orow
                                S.dma("sp", lambda e: e.dma_start(out=dst[r0:r0 + n, hf * 512:(hf + 1) * 512],
                                                                  in_=qf[j][0:n, :]), reads=[dk],
                                      writes=(["o_ks"] if orow == "s" else []))
                            if key0 is not None:
                                pending.append(lambda j=j, n=n, hf=hf, key0=key0: to_featmajor(qb[j], n, kT, hf * 4, key0))
                        elif key0 is not None:
                            pending.append(lambda j=j, n=n, hf=hf, key0=key0: to_featmajor(qb[j], n, qT, hf * 4, key0 - S_OWN))
                        elif orow == "s":
                            S.dma("sp", lambda e: e.dma_start(out=scr_q.ap()[0:1, hf * 512:(hf + 1) * 512],
                                                              in_=qf[j][0:1, :]), reads=[dk], writes=["scr_q"])
                    else:
                        if key0 is not None:
                            S.op("dve", lambda e: e.tensor_copy(out=vb[0:n, key0 // 128, hf * 512:(hf + 1) * 512],
                                                                in_=qf[j][0:n, :]),
                                 reads=[dk], writes=[("vb", key0, hf)])
                        if orow is not None:
                            dst = (o_vs if orow == "s" else o_v)
                            r0 = 0 if orow == "s" else orow
                            S.dma("sp", lambda e: e.dma_start(out=dst[r0:r0 + n, hf * 512:(hf + 1) * 512],
                                                              in_=qf[j][0:n, :]), reads=[dk],
                                  writes=(["o_vs"] if orow == "s" else []))
            flush(0)

        with ExitStack() as s1:
            kT = sbx(s1, "kT", [128, 8, 2 * S_OWN], BF16)
            qT = sbx(s1, "qT", [128, 8, S_OWN], BF16)
            vb = sbx(s1, "vb", [128, 16, 1024], BF16)
            pT = [sbx(s1, f"pT{i}", [128, 512], BF16) for i in range(4)]
            of = [sbx(s1, f"of{i}", [128, 512]) for i in range(4)]
            sqe = sbx(s1, "sqe", [128, 512], BF16)

            for t in range(NT):
                phase_a(x_ctx[t * 128:(t + 1) * 128, :], 128, uT, gn, 32 + t * 128, extra_last32=(t == NT - 1))
            qkv_pass([(128, 32 + t * 128, t, t * 128, None) for t in range(NT)], False, kT, qT, vb)
            for t in range(NT):
                phase_a(x_own[t * 128:(t + 1) * 128, :], 128, uT, gn, 32 + t * 128)
            phase_a(x_s[:, :], NS, uT, gn, 32 + S_OWN)
            tiles = [(128, 32 + t * 128, 8 + t, S_OWN + t * 128, t * 128) for t in range(NT)]
            tiles.append((NS, 32 + S_OWN, 16, None, "s"))
            qkv_pass(tiles, True, kT, qT, vb)

            if stage >= 4:
                LA = 2
                its = []
                for h in range(8):
                    for j in range(2):
                        nkb = 8 + 4 * j + 4
                        for kb in range(nkb):
                            own_i = kb - 8
                            off, diag = 0, False
                            if own_i >= 4 * j:
                                off, diag = (own_i - 4 * j) * 128, True
                            for m in range(2):
                                its.append((h, j, kb, m, off, diag, nkb, h * 2 + j))

                def stage_a(ix):
                    h, j, kb, m, off, diag, nkb, gi = its[ix]
                    sb_i = ix % 4
                    N = 512 - off
                    q0 = j * 512 + off
                    S.group("pe", [lambda e: e.matmul(
                        ps[sb_i][:, 0:N], kT[m * 64:(m + 1) * 64, h, kb * 128:(kb + 1) * 128],
                        qT[m * 64:(m + 1) * 64, h, q0:q0 + N], start=True, stop=True)],
                        writes=[PK(sb_i)])
                    bias = cbias[:, 0:1] if kb < 8 else 0.0
                    S.op("act", lambda e: e.activation(
                        out=pT[sb_i][:, 0:N], in_=ps[sb_i][:, 0:N], func=AF.Exp, scale=0.125, bias=bias),
                        reads=["cbias"], writes=[PK(sb_i), ("pT", sb_i)])
                    if diag:
                        S.op("pool", lambda e: e.tensor_tensor(
                            out=pT[sb_i][:, 0:128], in0=pT[sb_i][:, 0:128], in1=trib[:], op=ALU.mult),
                            reads=["trib"], writes=[("pT", sb_i)])

                def lacc_of(gi, m):
                    c0 = ((gi % 2) * 2 + m) * 512
                    return xa[0][:, c0:c0 + 512]

                def obank(gi, m):
                    return 4 + 2 * (gi % 2) + m

                def stage_b(ix):
                    h, j, kb, m, off, diag, nkb, gi = its[ix]
                    sb_i = ix % 4
                    N = 512 - off
                    ob = obank(gi, m)
                    la = lacc_of(gi, m)
                    lk = ("lacc", gi % 2, m)
                    S.group("pe", [
                        lambda e: e.matmul(ps[ob][:, off:512], vb[:, kb, h * 128:(h + 1) * 128], pT[sb_i][:, 0:N],
                                           start=(kb == 0), stop=(kb == nkb - 1))],
                        reads=[("pT", sb_i)], writes=[PK(ob)])
                    ae = "pool" if m == 0 else "dve"
                    if kb == 0:
                        S.op(ae, lambda e: e.tensor_copy(out=la, in_=pT[sb_i][:, 0:512]),
                             reads=[("pT", sb_i)], writes=[lk])
                    else:
                        S.op(ae, lambda e: e.tensor_tensor(out=la[:, off:512], in0=la[:, off:512],
                                                           in1=pT[sb_i][:, 0:N], op=ALU.add),
                             reads=[("pT", sb_i)], writes=[lk])
                    if kb == nkb - 1 and m == 1:
                        epilogue(h, j, gi)

                def epilogue(h, j, gi):
                    o0, o1 = obank(gi, 0), obank(gi, 1)
                    S.group("pe", [lambda e, m=m: e.matmul(ps[m][:], onesf128[:], lacc_of(gi, m), start=True, stop=True)
                                   for m in range(2)],
                            reads=[("lacc", gi % 2, 0), ("lacc", gi % 2, 1), "onesf128"], writes=[PK(0), PK(1)])
                    for m_ in range(2):
                        S.op("act", lambda e, m_=m_: e.activation(out=of[m_][:], in_=ps[m_][:], func=AF.Ln),
                             writes=[PK(m_), f"of{m_}"])
                        S.op("act", lambda e, m_=m_: e.activation(out=of[m_][:], in_=of[m_][:], func=AF.Exp, scale=-1.0),
                             writes=[f"of{m_}"])
                    S.op("dve", lambda e: e.tensor_tensor(out=of[0][:], in0=ps[o0][:], in1=of[0][:], op=ALU.mult),
                         writes=[PK(o0), "of0"])
                    S.op("dve", lambda e: e.tensor_tensor(out=of[1][:], in0=ps[o1][:], in1=of[1][:], op=ALU.mult),
                         writes=[PK(o1), "of1"])
                    S.op("dve", lambda e: e.scalar_tensor_tensor(out=of[2][:], in0=of[1][:], scalar=lam[:, 3:4],
                                                                 in1=of[0][:], op0=ALU.mult, op1=ALU.add),
                         reads=["of0", "of1", "lam"], writes=["of2"])
                    S.op("dve", lambda e: e.tensor_tensor(out=sqe[:], in0=of[2][:], in1=of[2][:], op=ALU.mult),
                         reads=["of2"], writes=["sqe"])
                    S.group("pe", [lambda e: e.matmul(ps[2][:], onesb[:], sqe[:], start=True, stop=True)],
                            reads=["sqe", "onesb"], writes=[PK(2)])
                    S.op("act", lambda e: e.activation(out=of[3][:], in_=ps[2][:], func=AF.Ln, scale=1.0 / 128,
                                                       bias=epsb[:, 0:1]), reads=["epsb"], writes=[PK(2), "of3"])
                    S.op("act", lambda e: e.activation(out=of[3][:], in_=of[3][:], func=AF.Exp, scale=-0.5), writes=["of3"])
                    S.op("dve", lambda e: e.tensor_tensor(out=of[2][:], in0=of[2][:], in1=of[3][:], op=ALU.mult),
                         reads=["of3"], writes=["of2"])
                    S.op("dve", lambda e: e.tensor_scalar(
                        out=oT[:, h, j * 512:(j + 1) * 512], in0=of[2][:], scalar1=gsub[:, 0:1],
                        scalar2=1.0 - LAM_INIT, op0=ALU.mult, op1=ALU.mult),
                        reads=["of2", "gsub"], writes=[("oT", h)])

                n_it = len(its)
                for p_ in range(n_it // 2 + 1):
                    if 2 * p_ < n_it:
                        stage_a(2 * p_)
                        stage_a(2 * p_ + 1)
                    if p_ >= 1:
                        stage_b(2 * (p_ - 1))
                        stage_b(2 * (p_ - 1) + 1)
            S.barrier()
        if True:
            if stage >= 5 and sub != 5:
                with ExitStack() as s1b:
                    kc = [sbx(s1b, f"kc{i}", [128, 4096]) for i in range(2)]
                    vc = [sbx(s1b, f"vc{i}", [128, 2048], BF16) for i in range(2)]
                    kprod = sbx(s1b, "kprod", [128, 4096], BF16)
                    ptT = sbx(s1b, "ptT", [128, 1], I32)
                    idx32 = sbx(s1b, "idx32", [128, 32], I32)
                    idx64 = sbx(s1b, "idx64", [128, 64], I32)
                    qkvr = sbx(s1b, "qkvr", [1, 3072])
                    vrb = sbx(s1b, "vrb", [1, 1024], BF16)
                    onesrow = sbx(s1b, "onesrow", [1, 128])
                    zrow = sbx(s1b, "zrow", [1, 512], BF16)
                    qrep = sbx(s1b, "qrep", [128, 1024])
                    sS = sbx(s1b, "sS", [128, 2048])
                    pS = sbx(s1b, "pS", [128, 2048], BF16)
                    lsum = sbx(s1b, "lsum", [128, 16])
                    qk1 = sbx(s1b, "qk1", [1, 1024])
                    pself = sbx(s1b, "pself", [1, 16])
                    pselfb = sbx(s1b, "pselfb", [1, 16], BF16)
                    rL = sbx(s1b, "rL", [2, 8])
                    on = sbx(s1b, "on", [2, 1024])
                    osr = sbx(s1b, "osr", [1, 1024])
                    osq = sbx(s1b, "osq", [1, 1024])
                    ors = sbx(s1b, "ors", [1, 8])
                    osTb = sbx(s1b, "osTb", [8, 1024], BF16)

                    S.dma("sp", lambda e: e.dma_start(out=ptT[:], in_=pt_T), writes=["ptT"])
                    for cc in range(32):
                        S.op("dve", lambda e, cc=cc: e.tensor_scalar(out=idx32[:, cc:cc + 1], in0=ptT[:], scalar1=32.0,
                                                                     scalar2=float(cc), op0=ALU.mult, op1=ALU.add),
                             reads=["ptT"], writes=["idx32"])
                    for cc in range(64):
                        S.op("dve", lambda e, cc=cc: e.tensor_scalar(out=idx64[:, cc:cc + 1], in0=ptT[:], scalar1=64.0,
                                                                     scalar2=float(cc), op0=ALU.mult, op1=ALU.add),
                             reads=["ptT"], writes=["idx64"])
                    S.op("dve", lambda e: e.memset(onesrow[:], 1.0), writes=["onesrow"])
                    S.dma("sp", lambda e: e.dma_start(out=qkvr[:, 0:1024], in_=scr_q.ap()), reads=["scr_q"], writes=["qkvr"])
                    S.dma("sp", lambda e: e.dma_start(out=qkvr[:, 1024:2048], in_=o_ks[0:1, :]), reads=["o_ks"], writes=["qkvr"])
                    S.dma("sp", lambda e: e.dma_start(out=qkvr[:, 2048:3072], in_=o_vs[0:1, :]), reads=["o_vs"], writes=["qkvr"])
                    S.op("act", lambda e: e.activation(out=vrb[:], in_=qkvr[:, 2048:3072], func=AF.Copy),
                         reads=["qkvr"], writes=["vrb"])
                    S.op("dve", lambda e: e.tensor_tensor(out=qk1[:], in0=qkvr[:, 0:1024], in1=qkvr[:, 1024:2048],
                                                          op=ALU.mult), reads=["qkvr"], writes=["qk1"])
                    S.op("dve", lambda e: e.tensor_reduce(out=pself[:], in_=qk1[:].rearrange("p (g d) -> p g d", d=64),
                                                          axis=AX.X, op=ALU.add), reads=["qk1"], writes=["pself"])
                    S.op("act", lambda e: e.activation(out=pself[:], in_=pself[:], func=AF.Exp, scale=0.125),
                         writes=["pself"])
                    S.op("dve", lambda e: e.tensor_copy(out=pselfb[:], in_=pself[:]), reads=["pself"], writes=["pselfb"])
                    for g in range(2):
                        pi = next_ps()
                        S.group("pe", [lambda e, g=g, pi=pi: e.matmul(ps[pi][:, :], onesrow[0:1, :],
                                                                      qkvr[0:1, g * 512:(g + 1) * 512], start=True, stop=True)],
                                reads=["onesrow", "qkvr"], writes=[PK(pi)])
                        S.op("act", lambda e, g=g, pi=pi: e.activation(out=qrep[:, g * 512:(g + 1) * 512], in_=ps[pi][:, :],
                                                                       func=AF.Copy), writes=[PK(pi), "qrep"])
                    for cc in range(32):
                        i = cc % 2
                        S.dma("pool", lambda e, cc=cc, i=i: e.indirect_dma_start(
                            out=kc[i][:], out_offset=None, in_=cache_k4[:, :],
                            in_offset=bass.IndirectOffsetOnAxis(ap=idx32[:, cc:cc + 1], axis=0)),
                            reads=["idx32"], writes=[("kc", i)])
                        kv = kc[i][:].rearrange("p (t d) -> p t d", d=1024)
                        S.op("dve", lambda e, kv=kv: e.tensor_tensor(
                            out=kprod[:].rearrange("p (t d) -> p t d", d=1024), in0=kv,
                            in1=qrep[:].unsqueeze(1).to_broadcast([128, 4, 1024]), op=ALU.mult),
                            reads=["qrep", ("kc", i)], writes=["kprod"])
                        S.op("dve", lambda e, cc=cc: e.tensor_reduce(
                            out=sS[:, cc * 64:(cc + 1) * 64], in_=kprod[:].rearrange("p (g d) -> p g d", d=64),
                            axis=AX.X, op=ALU.add), reads=["kprod"], writes=["sS"])
                    S.op("act", lambda e: e.activation(out=pS[:], in_=sS[:], func=AF.Exp, scale=0.125),
                         reads=["sS"], writes=["pS"])
                    S.op("dve", lambda e: e.tensor_reduce(
                        out=lsum[:], in_=pS[:].rearrange("p (t g) -> p g t", g=16), axis=AX.X, op=ALU.add),
                        reads=["pS"], writes=["lsum"])
                    pa, pb_, pl = next_ps(), next_ps(), next_ps()
                    S.op("dve", lambda e: e.memset(zrow[:], 0.0), writes=["zrow"])
                    S.group("pe", [lambda e, bank=bank: e.matmul(ps[bank][0:2, :], zrow[0:1, 0:2], zrow[0:1, 0:512],
                                                                 start=True, stop=False) for bank in (pa, pb_)],
                            reads=["zrow"], writes=[PK(pa), PK(pb_)])
                    for cc in range(64):
                        i = cc % 2
                        S.dma("pool", lambda e, cc=cc, i=i: e.indirect_dma_start(
                            out=vc[i][:], out_offset=None, in_=cache_v8[:, :],
                            in_offset=bass.IndirectOffsetOnAxis(ap=idx64[:, cc:cc + 1], axis=0)),
                            reads=["idx64"], writes=[("vc", i)])
                        fns = []
                        for tl in range(2):
                            t = cc * 2 + tl
                            for h in range(8):
                                bank = pa if h < 4 else pb_
                                fns.append(lambda e, t=t, tl=tl, h=h, i=i, bank=bank: e.matmul(
                                    ps[bank][0:2, (h % 4) * 128:(h % 4 + 1) * 128], pS[:, t * 16 + h * 2:t * 16 + h * 2 + 2],
                                    vc[i][:, tl * 1024 + h * 128:tl * 1024 + (h + 1) * 128], start=False, stop=False))
                        S.group("pe", fns, reads=[("vc", i), "pS"], writes=[PK(pa), PK(pb_)])
                    for h in range(8):
                        bank = pa if h < 4 else pb_
                        S.group("pe", [lambda e, h=h, bank=bank: e.matmul(
                            ps[bank][0:2, (h % 4) * 128:(h % 4 + 1) * 128], pselfb[0:1, h * 2:h * 2 + 2],
                            vrb[0:1, h * 128:(h + 1) * 128], start=False, stop=(h % 4 == 3))],
                            reads=["pselfb", "vrb"], writes=[PK(bank)])
                        S.group("pe", [
                            lambda e, h=h: e.matmul(ps[pl][0:2, 2 * h:2 * h + 2], lsum[:, h * 2:h * 2 + 2], onesf[:, 0:2],
                                                    start=True, stop=False),
                            lambda e, h=h: e.matmul(ps[pl][0:2, 2 * h:2 * h + 2], pself[0:1, h * 2:h * 2 + 2], onesf[0:1, 0:2],
                                                    start=False, stop=True)],
                            reads=["lsum", "pself", "onesf"], writes=[PK(pl)])
                    S.op("dve", lambda e: e.reciprocal(out=rL[:], in_=ps[pl][0:2, 0:16].rearrange("p (b t) -> p b t", t=2)[:, :, 0]),
                         writes=[PK(pl), "rL"])
                    for g, bank in ((0, pa), (1, pb_)):
                        S.op("dve", lambda e, g=g, bank=bank: e.tensor_tensor(
                            out=on[:, g * 512:(g + 1) * 512].rearrange("p (b d) -> p b d", d=128),
                            in0=ps[bank][0:2, :].rearrange("p (b d) -> p b d", d=128),
                            in1=rL[:, g * 4:(g + 1) * 4].unsqueeze(2).to_broadcast([2, 4, 128]), op=ALU.mult),
                            reads=["rL"], writes=[PK(bank), ("on", g)])
                    for g in range(2):
                        pi = next_ps()
                        S.group("pe", [lambda e, g=g, pi=pi: e.matmul(ps[pi][0:1, :], coef[:], on[:, g * 512:(g + 1) * 512],
                                                                      start=True, stop=True)],
                                reads=["coef", ("on", g)], writes=[PK(pi)])
                        S.op("act", lambda e, g=g, pi=pi: e.activation(out=osr[:, g * 512:(g + 1) * 512], in_=ps[pi][0:1, :],
                                                                       func=AF.Copy), writes=[PK(pi), "osr"])
                    S.op("dve", lambda e: e.tensor_tensor(out=osq[:], in0=osr[:], in1=osr[:], op=ALU.mult),
                         reads=["osr"], writes=["osq"])
                    S.op("dve", lambda e: e.tensor_reduce(out=ors[:], in_=osq[:].rearrange("p (b d) -> p b d", d=128),
                                                          axis=AX.X, op=ALU.add), reads=["osq"], writes=["ors"])
                    S.op("act", lambda e: e.activation(out=ors[:], in_=ors[:], func=AF.Sqrt, scale=1.0 / 128,
                                                       bias=epsb[0:1, 0:1]), reads=["epsb"], writes=["ors"])
                    S.op("dve", lambda e: e.reciprocal(out=ors[:], in_=ors[:]), writes=["ors"])
                    osr3 = osr[:].rearrange("p (b d) -> p b d", d=128)
                    S.op("dve", lambda e: e.tensor_tensor(out=osr3, in0=osr3,
                                                          in1=ors[:].unsqueeze(2).to_broadcast([1, 8, 128]), op=ALU.mult),
                         reads=["ors"], writes=["osr"])
                    S.op("dve", lambda e: e.tensor_tensor(out=osr3, in0=osr3,
                                                          in1=gsubrow[:].unsqueeze(1).to_broadcast([1, 8, 128]), op=ALU.mult),
                         reads=["gsubrow"], writes=["osr"])
                    S.op("dve", lambda e: e.memset(osTb[:], 0.0), writes=[("dstb", id(osTb))])
                    S.op("dve", lambda e: e.tensor_scalar(out=osTb[0:1, :], in0=osr[:], scalar1=1.0 - LAM_INIT, scalar2=None,
                                                          op0=ALU.mult), reads=["osr"], writes=[("dstb", id(osTb))])
                    to_featmajor(osTb, NS, oT, 0, S_OWN, nh=8)
            S.barrier()
        if sub == 5 or stage < 5:
            S.op("dve", lambda e: e.memset(oT[:, :, S_OWN:NTOK], 0.0), writes=[("tgt", id(oT), oc_) for oc_ in range(8)])
        ocvT = sb("ocvT", [128, 8, NTOK], BF16)
        if stage >= 6:
          with ExitStack() as s2:
            gT = sbx(s2, "gT", [128, 8, 32 + NTOK], BF16)
            cT = sbx(s2, "cT", [128, 8, NTOK], BF16)
            acc = [sbx(s2, f"acc{i}", [128, NTOK]) for i in range(2)]
            sqb = [sbx(s2, f"sqb{i}", [128, NTOK], BF16) for i in range(2)]
            mu = sbx(s2, "mu", [128, NTOK])
            var = sbx(s2, "var", [128, NTOK])
            histT = sbx(s2, "histT", [128, 8, 8, 31])
            sct = sbx(s2, "sct", [30, 1024])
            tmpf = [sbx(s2, f"tmpf{i}", [128, 512]) for i in range(2)]
            tmpb = [sbx(s2, f"tmpb{i}", [128, 512], BF16) for i in range(2)]
            cvo = sbx(s2, "cvo", [30, 1024])
            cso = sbx(s2, "cso", [8, 1024])
            csum = sbx(s2, "csum", [128, 8, 8])

            tn = [0]
            own_chunks = [(32, 512), (32 + 512, 512), (32 + 1024, NS)]

            def gate_mult(col0, target):
                for hf2 in range(2):
                    wi, wbuf = load_wslab(w_in, col0 + hf2 * 512, 512)
                    for c4 in range(4):
                        oc = hf2 * 4 + c4
                        for (u0, N) in own_chunks:
                            pi = next_ps()
                            S.group("pe", [lambda e, c=c, pi=pi, u0=u0, N=N, c4=c4, wbuf=wbuf: e.matmul(
                                ps[pi][:, 0:N], wbuf[:, c, c4 * 128:(c4 + 1) * 128], uT[:, c, u0:u0 + N],
                                start=(c == 0), stop=(c == 15)) for c in range(16)],
                                reads=[("wb", wi), ("uT", "all")], writes=[PK(pi)])
                            i = tn[0] % 2
                            tn[0] += 1
                            S.op("act", lambda e, pi=pi, i=i, N=N: e.activation(out=tmpb[i][:, 0:N], in_=ps[pi][:, 0:N],
                                                                               func=AF.Silu),
                                 writes=[PK(pi), ("tmpb", i)])
                            t0 = u0 - 32
                            S.op("dve", lambda e, i=i, N=N, oc=oc, t0=t0: e.tensor_tensor(
                                out=target[:, oc, t0:t0 + N], in0=target[:, oc, t0:t0 + N], in1=tmpb[i][:, 0:N],
                                op=ALU.mult), reads=[("tmpb", i)], writes=[("tgt", id(target), oc)])

            def gate_chunk(wbuf, wkey, wc0, target, oc):
                for (u0, N) in own_chunks:
                    pi = next_ps()
                    S.group("pe", [lambda e, c=c, pi=pi, u0=u0, N=N: e.matmul(
                        ps[pi][:, 0:N], wbuf[:, c, wc0:wc0 + 128], uT[:, c, u0:u0 + N],
                        start=(c == 0), stop=(c == 15)) for c in range(16)],
                        reads=[wkey, ("uT", "all")], writes=[PK(pi)])
                    i = tn[0] % 2
                    tn[0] += 1
                    S.op("act", lambda e, pi=pi, i=i, N=N: e.activation(out=tmpb[i][:, 0:N], in_=ps[pi][:, 0:N],
                                                                       func=AF.Silu),
                         writes=[PK(pi), ("tmpb", i)])
                    t0 = u0 - 32
                    S.op("pool", lambda e, i=i, N=N, t0=t0: e.tensor_tensor(
                        out=target[:, oc, t0:t0 + N], in0=target[:, oc, t0:t0 + N], in1=tmpb[i][:, 0:N],
                        op=ALU.mult), reads=[("tmpb", i)], writes=[("tgt", id(target), oc)])

            xa_w = xa[0][:].bitcast(BF16).rearrange("p (c n) -> p c n", n=256)

            def load_gate_a(col0):
                src = w_in[:, col0:col0 + 256].rearrange("(c p) n -> p c n", p=128)
                for c0 in (0, 8):
                    S.dma("pool", lambda e, c0=c0: e.dma_start(out=xa_w[:, c0:c0 + 8, :], in_=src[:, c0:c0 + 8, :]),
                          writes=[("xa", 0)])

            for b in range(NS):
                S.dma("sp", lambda e, b=b: e.dma_start(out=sct[:], in_=state_conv[b, :, :]), writes=["sct"])
                pi = next_ps()
                pv = ps[pi][:, 0:240].rearrange("p (c k) -> p c k", k=30)
                S.group("pe", [lambda e, oc=oc, pv=pv: e.transpose(pv[:, oc, :], sct[:, oc * 128:(oc + 1) * 128],
                                                                    identf[0:30, 0:30]) for oc in range(8)],
                        reads=["sct", "identf"], writes=[PK(pi)])
                S.op("act", lambda e, b=b, pv=pv: e.activation(out=histT[:, :, b, 0:30], in_=pv, func=AF.Copy),
                     writes=[PK(pi), "histT"])
            S.op("dve", lambda e: e.tensor_tensor(out=histT[:, :, :, 0:30], in0=histT[:, :, :, 0:30],
                                                  in1=wdw[:, :, 0:30].unsqueeze(2).to_broadcast([128, 8, 8, 30]), op=ALU.mult),
                 reads=["wdw"], writes=["histT"])
            S.op("dve", lambda e: e.tensor_reduce(out=csum[:], in_=histT[:, :, :, 0:30], axis=AX.X, op=ALU.add),
                 reads=["histT"], writes=["csum"])

            st_chunks = [(0, 512), (512, 512), (1024, NS)]

            def conv_taps(oc):
                a = acc[oc % 2]
                ak = ("acc", oc % 2)
                S.op("dve", lambda e: e.tensor_scalar(
                    out=a[:, 0:S_OWN], in0=gT[:, oc, 2:2 + S_OWN], scalar1=wdw[:, oc, 0:1], scalar2=vecs[:, 0, oc:oc + 1],
                    op0=ALU.mult, op1=ALU.add), reads=[("gT", oc), "wdw", "vecs"], writes=[ak])
                for k in range(1, 31):
                    S.op("dve", lambda e, k=k: e.scalar_tensor_tensor(
                        out=a[:, 0:S_OWN], in0=gT[:, oc, 2 + k:2 + k + S_OWN], scalar=wdw[:, oc, k:k + 1],
                        in1=a[:, 0:S_OWN], op0=ALU.mult, op1=ALU.add), reads=[("gT", oc), "wdw"], writes=[ak])
                S.op("dve", lambda e: e.scalar_tensor_tensor(
                    out=a[:, S_OWN:NTOK], in0=gT[:, oc, 32 + S_OWN:32 + S_OWN + NS], scalar=wdw[:, oc, 30:31],
                    in1=csum[:, oc, :], op0=ALU.mult, op1=ALU.add), reads=[("gT", oc), "csum", "wdw"], writes=[ak])
                S.op("dve", lambda e: e.tensor_scalar(
                    out=a[:, S_OWN:NTOK], in0=a[:, S_OWN:NTOK], scalar1=vecs[:, 0, oc:oc + 1], scalar2=None, op0=ALU.add),
                    reads=["vecs"], writes=[ak])
                S.op("act", lambda e: e.activation(out=cT[:, oc, :], in_=a[:, :], func=AF.Copy),
                     reads=[ak], writes=[("cT", oc)])

            glu_chunks = [(0, 512), (512, 512), (1024, 32 + NTOK - 1024)]
            for hf2 in range(2):
                wia, wba = load_wslab(w_in, 4096 + hf2 * 512, 512)
                wib, wbb = load_wslab(w_in, 5120 + hf2 * 512, 512)
                for c4 in range(4):
                    oc = hf2 * 4 + c4
                    for (u0, N) in glu_chunks:
                        pa_, pb2 = next_ps(), next_ps()
                        S.group("pe", [lambda e, c=c, u0=u0, N=N, c4=c4, pa_=pa_, wba=wba: e.matmul(
                            ps[pa_][:, 0:N], wba[:, c, c4 * 128:(c4 + 1) * 128], uT[:, c, u0:u0 + N],
                            start=(c == 0), stop=(c == 15)) for c in range(16)],
                            reads=[("wb", wia), ("uT", "all")], writes=[PK(pa_)])
                        S.group("pe", [lambda e, c=c, u0=u0, N=N, c4=c4, pb2=pb2, wbb=wbb: e.matmul(
                            ps[pb2][:, 0:N], wbb[:, c, c4 * 128:(c4 + 1) * 128], uT[:, c, u0:u0 + N],
                            start=(c == 0), stop=(c == 15)) for c in range(16)],
                            reads=[("wb", wib), ("uT", "all")], writes=[PK(pb2)])
                        i = tn[0] % 2
                        tn[0] += 1
                        S.op("act", lambda e, pb2=pb2, i=i, N=N: e.activation(out=tmpf[i][:, 0:N], in_=ps[pb2][:, 0:N],
                                                                             func=AF.Sigmoid),
                             writes=[PK(pb2), ("tmpf", i)])
                        S.op("act", lambda e, pa_=pa_, N=N, oc=oc, u0=u0: e.activation(
                            out=gT[:, oc, u0:u0 + N], in_=ps[pa_][:, 0:N], func=AF.Copy),
                            writes=[PK(pa_), ("gT", oc)])
                        S.op("pool", lambda e, i=i, N=N, oc=oc, u0=u0: e.tensor_tensor(
                            out=gT[:, oc, u0:u0 + N], in0=gT[:, oc, u0:u0 + N], in1=tmpf[i][:, 0:N], op=ALU.mult),
                            reads=[("tmpf", i)], writes=[("gT", oc)])
                    if c4 % 2 == 0:
                        load_gate_a(3072 + oc * 128)
                    gate_chunk(xa_w, ("xa", 0), (c4 % 2) * 128, oT, oc)
                    conv_taps(oc)

            for (c0, n, dst_sb, dst_dram, key) in ((32 + S_OWN - 30, 30, cvo, o_conv[:, :], "cvo"),
                                                   (32 + S_OWN, NS, cso, None, "cso")):
                pi = next_ps()
                pst = ps[pi][:].bitcast(BF16).rearrange("p (c t) -> p c t", c=8)
                S.group("pe", [lambda e, oc=oc, c0=c0, n=n, pst=pst: e.transpose(pst[0:n, oc, :], gT[:, oc, c0:c0 + n], identb[:])
                               for oc in range(8)], reads=[("gT", oc) for oc in range(8)] + ["identb"], writes=[PK(pi)])
                S.op("act", lambda e, n=n, dst_sb=dst_sb, pst=pst: e.activation(
                    out=dst_sb[0:n, :].rearrange("p (c t) -> p c t", c=8), in_=pst[0:n, :, :], func=AF.Copy),
                    writes=[PK(pi), key])
                if dst_dram is not None:
                    S.dma("sp", lambda e, dst_dram=dst_dram, dst_sb=dst_sb, n=n: e.dma_start(out=dst_dram, in_=dst_sb[0:n, :]),
                          reads=[key])
            S.dma("sp", lambda e: e.dma_start(out=o_convs[:, 29, :], in_=cso[0:NS, :]), reads=["cso"])
            S.dma("sp", lambda e: e.dma_start(out=o_convs[:, 0:29, :], in_=state_conv[:, 1:30, :]))

            ps_s1 = [next_ps(), next_ps(), next_ps()]
            ps_s2 = [next_ps(), next_ps(), next_ps()]
            for oc in range(8):
                i = oc % 2
                S.op("act", lambda e, oc=oc, i=i: e.activation(out=sqb[i][:, :], in_=cT[:, oc, :], func=AF.Square),
                     reads=[("cT", oc)], writes=[("sqb", i)])
                for ci, (t0, N) in enumerate(st_chunks):
                    S.group("pe", [
                        lambda e, ci=ci, t0=t0, N=N, oc=oc: e.matmul(ps[ps_s1[ci]][:, 0:N], onesb[:], cT[:, oc, t0:t0 + N],
                                                                     start=(oc == 0), stop=(oc == 7)),
                        lambda e, ci=ci, t0=t0, N=N, oc=oc, i=i: e.matmul(ps[ps_s2[ci]][:, 0:N], onesb[:], sqb[i][:, t0:t0 + N],
                                                                          start=(oc == 0), stop=(oc == 7))],
                        reads=[("cT", oc), ("sqb", i), "onesb"], writes=[PK(ps_s1[ci]), PK(ps_s2[ci])])
            for ci, (t0, N) in enumerate(st_chunks):
                S.op("act", lambda e, ci=ci, t0=t0, N=N: e.activation(out=mu[:, t0:t0 + N], in_=ps[ps_s1[ci]][:, 0:N],
                                                                     func=AF.Copy, scale=1.0 / 1024),
                     writes=[PK(ps_s1[ci]), "mu"])
                S.op("act", lambda e, ci=ci, t0=t0, N=N: e.activation(out=var[:, t0:t0 + N], in_=ps[ps_s2[ci]][:, 0:N],
                                                                     func=AF.Copy, scale=1.0 / 1024),
                     writes=[PK(ps_s2[ci]), "var"])
            S.op("dve", lambda e: e.tensor_tensor(out=acc[0][:], in0=mu[:], in1=mu[:], op=ALU.mult),
                 reads=["mu"], writes=[("acc", 0)])
            S.op("dve", lambda e: e.tensor_tensor(out=var[:], in0=var[:], in1=acc[0][:], op=ALU.subtract),
                 reads=[("acc", 0)], writes=["var"])
            S.op("act", lambda e: e.activation(out=var[:], in_=var[:], func=AF.Sqrt, bias=epsb[:, 0:1]),
                 reads=["epsb"], writes=["var"])
            S.op("dve", lambda e: e.reciprocal(out=var[:], in_=var[:]), writes=["var"])
            for oc in range(8):
                i = oc % 2
                S.op("dve", lambda e, oc=oc, i=i: e.tensor_tensor(out=acc[i][:], in0=cT[:, oc, :], in1=mu[:], op=ALU.subtract),
                     reads=[("cT", oc), "mu"], writes=[("acc", i)])
                S.op("dve", lambda e, i=i: e.tensor_tensor(out=acc[i][:], in0=acc[i][:], in1=var[:], op=ALU.mult),
                     reads=["var"], writes=[("acc", i)])
                S.op("act", lambda e, oc=oc, i=i: e.activation(out=cT[:, oc, :], in_=acc[i][:], func=AF.Silu,
                                                               scale=vecs[:, 1, oc:oc + 1], bias=vecs[:, 2, oc:oc + 1]),
                     reads=[("acc", i), "vecs"], writes=[("cT", oc)])
            wi, wbuf = load_wslab(w_pw, 0, 1024, nchunk=8)
            for oc in range(8):
                for (t0, N) in st_chunks:
                    pi = next_ps()
                    S.group("pe", [lambda e, ic=ic, oc=oc, t0=t0, N=N, pi=pi: e.matmul(
                        ps[pi][:, 0:N], wbuf[:, ic, oc * 128:(oc + 1) * 128], cT[:, ic, t0:t0 + N],
                        start=(ic == 0), stop=(ic == 7)) for ic in range(8)],
                        reads=[("wb", wi)] + [("cT", ic) for ic in range(8)], writes=[PK(pi)])
                    S.op("act", lambda e, oc=oc, t0=t0, N=N, pi=pi: e.activation(out=ocvT[:, oc, t0:t0 + N], in_=ps[pi][:, 0:N],
                                                                                func=AF.Copy),
                         writes=[PK(pi), ("tgt", id(ocvT), oc)])
            gate_mult(6144, ocvT)
            S.barrier()

        if stage >= 7:
          with ExitStack() as s3:
            hb = sbx(s3, "hb", [128, 9, D])
            gfin = wb[0][:].rearrange("p c n -> p (c n)")[:, 0:4096].bitcast(F32)
            pT2 = sbx(s3, "ppT2", [128, 2, NTOK], BF16)
            pin = sbx(s3, "pin", [128, 256])
            pinb = sbx(s3, "pinb", [128, 256], BF16)
            sg = [sbx(s3, f"sg{i}", [128, 512]) for i in range(2)]
            tl = [(128, t * 128, x_own[t * 128:(t + 1) * 128, :], p_own[t * 128:(t + 1) * 128, :], o_y[t * 128:(t + 1) * 128, :])
                  for t in range(NT)]
            tl.append((NS, S_OWN, x_s[:, :], p_s[:, :], o_ys[:, :]))
            for ti, (n, t0, xap, pap, yap) in enumerate(tl):
                S.dma("sp", lambda e, ti=ti, n=n, xap=xap: e.dma_start(out=hb[0:n, ti, :], in_=xap), writes=[("hb", ti)])
            for sl in range(4):
                wi, wbuf = load_wslab(w_out, sl * 512, 512)
                for ti, (n, t0, xap, pap, yap) in enumerate(tl):
                    pi = mm_tok(wbuf, wi, n, lambda c, t0=t0, n=n: (oT[:, c, t0:t0 + n] if c < 8 else ocvT[:, c - 8, t0:t0 + n]))
                    S.op("dve", lambda e, ti=ti, n=n, sl=sl, pi=pi: e.tensor_tensor(
                        out=hb[0:n, ti, sl * 512:(sl + 1) * 512], in0=ps[pi][0:n, :], in1=hb[0:n, ti, sl * 512:(sl + 1) * 512],
                        op=ALU.add), writes=[PK(pi), ("hb", ti)])
            for ti, (n, t0, xap, pap, yap) in enumerate(tl):
                phase_a(None, n, uT, gple, 32 + t0, keep_x=(hb[0:n, ti, :], ("hb", ti)))
                S.dma("sp", lambda e, n=n, pap=pap: e.dma_start(out=pin[0:n, :], in_=pap), writes=["pin"])
                S.op("act", lambda e, n=n: e.activation(out=pinb[0:n, :], in_=pin[0:n, :], func=AF.Copy), reads=["pin"],
                     writes=[("dstb", id(pinb))])
                to_featmajor(pinb, n, pT2, 0, t0, nh=2)
            for sl in range(4):
                wi, wbuf = load_wslab(w_pg, sl * 512, 512)
                wi2, wbuf2 = load_wslab(w_ple, sl * 512, 512, nchunk=2)
                for ti, (n, t0, xap, pap, yap) in enumerate(tl):
                    pi = mm_tok(wbuf, wi, n, lambda c, t0=t0, n=n: uT[:, c, 32 + t0:32 + t0 + n], extra=[("uT", 32 + t0)])
                    pi2 = mm_tok(wbuf2, wi2, n, lambda c, t0=t0, n=n: pT2[:, c, t0:t0 + n], nk=2,
                                 extra=[("T", id(pT2), 0, t0)])
                    i = (sl * 9 + ti) % 2
                    S.op("act", lambda e, n=n, pi=pi, i=i: e.activation(out=sg[i][0:n, :], in_=ps[pi][0:n, :], func=AF.Sigmoid),
                         writes=[PK(pi), ("sg", i)])
                    S.op("dve", lambda e, n=n, pi2=pi2, i=i: e.tensor_tensor(out=sg[i][0:n, :], in0=ps[pi2][0:n, :],
                                                                            in1=sg[i][0:n, :], op=ALU.mult),
                         writes=[PK(pi2), ("sg", i)])
                    S.op("dve", lambda e, ti=ti, n=n, sl=sl, i=i: e.tensor_tensor(
                        out=hb[0:n, ti, sl * 512:(sl + 1) * 512], in0=sg[i][0:n, :], in1=hb[0:n, ti, sl * 512:(sl + 1) * 512],
                        op=ALU.add), reads=[("sg", i)], writes=[("hb", ti)])
            S.dma("sp", lambda e: e.dma_start(out=gfin, in_=g_final_bc), writes=[("wb", 0)])
            for ti, (n, t0, xap, pap, yap) in enumerate(tl):
                hk = ("hb", ti)
                S.op("act", lambda e, ti=ti, n=n: e.activation(out=xs[0][0:n, :], in_=hb[0:n, ti, :], func=AF.Square,
                                                               accum_out=ss[0:n, 1:2]),
                     reads=[hk], writes=[("xs", 0), ("ss", 1)])
                S.op("act", lambda e, n=n: e.activation(out=rstd[0:n, 1:2], in_=ss[0:n, 1:2], func=AF.Sqrt, scale=1.0 / D,
                                                        bias=epsb[0:n, 0:1]), reads=[("ss", 1), "epsb"], writes=[("rstd", 1)])
                S.op("dve", lambda e, n=n: e.reciprocal(out=rstd[0:n, 1:2], in_=rstd[0:n, 1:2]), writes=[("rstd", 1)])
                S.op("act", lambda e, ti=ti, n=n: e.activation(out=xa[0][0:n, :], in_=hb[0:n, ti, :], func=AF.Copy,
                                                               scale=rstd[0:n, 1:2]),
                     reads=[hk, ("rstd", 1)], writes=[("xa", 0)])
                S.op("dve", lambda e, n=n: e.tensor_tensor(out=xa[0][0:n, :], in0=xa[0][0:n, :], in1=gfin[0:n, :], op=ALU.mult),
                     reads=[("wb", 0)], writes=[("xa", 0)])
                S.dma("sp", lambda e, n=n, yap=yap: e.dma_start(out=yap, in_=xa[0][0:n, :]), reads=[("xa", 0)])

        for t in S.all_tokens():
            S.wait("sp", t)
        print("ops:", S.cnt, S.dcnt, "waits:", S.nwaits)
    return nc


def _prep_inputs(inp):
    import ml_dtypes
    f = lambda k: np.asarray(inp[k], np.float32)
    xp = f("x_prompt")
    xsamp = f("x_sample").reshape(NS, D)
    pp = f("p_prompt")[0]
    psamp = f("p_sample")[0].reshape(NS, 256)
    w_in = np.ascontiguousarray(f("w_in")[0])
    ck = f("cache_k")[0]
    cv = f("cache_v")[0]
    pc16 = lambda v: np.ascontiguousarray(v.reshape(16, 128).T)
    pc8 = lambda v: np.ascontiguousarray(v.reshape(8, 128).T)
    ident = np.eye(128, dtype=np.float32)
    tri = (np.arange(128)[:, None] <= np.arange(128)[None, :]).astype(np.float32)
    common = {
        "w_in": w_in,
        "w_pw": np.ascontiguousarray(f("w_pw")[0]), "w_out": np.ascontiguousarray(f("w_out")[0]),
        "w_pg": np.ascontiguousarray(f("w_pg")[0]), "w_ple": np.ascontiguousarray(f("w_ple")[0]),
        "cache_k4": np.ascontiguousarray(ck).reshape(1280 * 32, 4096),
        "cache_v8": np.ascontiguousarray(cv).reshape(1280 * 64, 2048),
        "g_norm_pc": pc16(f("g_norm")[0]), "g_ple_pc": pc16(f("g_ple")[0]),
        "g_final_bc": np.ascontiguousarray(np.broadcast_to(f("g_final")[None, :], (128, D))),
        "g_subln_pc": np.ascontiguousarray(f("g_subln")[0].reshape(128, 1)),
        "g_subln_row": np.ascontiguousarray(f("g_subln")[0].reshape(1, 128)),
        "wdw_pc": np.ascontiguousarray(f("w_dw")[0].T.reshape(8, 128, 31).transpose(1, 0, 2)),
        "vec_pc": np.ascontiguousarray(np.stack([pc8(f("b_dw")[0]), pc8(f("g_cln")[0]), pc8(f("b_cln")[0])], axis=1)),
        "lam4": np.ascontiguousarray(np.broadcast_to(
            np.stack([f("lam_q1")[0], f("lam_k1")[0], f("lam_q2")[0], f("lam_k2")[0]])[None], (128, 4, 64))),
        "ident_b": ident.astype(ml_dtypes.bfloat16), "ident_f": ident, "tri_b": tri.astype(ml_dtypes.bfloat16),
        "sel2": np.array([[1.0, 0.0], [0.0, -1.0]], np.float32),
    }
    maps = []
    for c in range(8):
        b, hf = c // 2, c % 2
        x_own = np.ascontiguousarray(xp[b, hf * S_OWN:(hf + 1) * S_OWN])
        x_ctx = np.ascontiguousarray(xp[b, 0:S_OWN]) if hf == 1 else np.zeros((S_OWN, D), np.float32)
        pos = np.concatenate([np.arange(S_OWN) + (hf - 1) * S_OWN, np.arange(S_OWN) + hf * S_OWN,
                              np.full(128, PAST)]).astype(np.float32)
        cos, sin = _rope_tables(pos)
        m = dict(common)
        m.update({
            "x_ctx": x_ctx, "x_own": x_own,
            "p_own": np.ascontiguousarray(pp[b, hf * S_OWN:(hf + 1) * S_OWN]),
            "x_s": np.ascontiguousarray(np.roll(xsamp, -c, axis=0)),
            "p_s": np.ascontiguousarray(np.roll(psamp, -c, axis=0)),
            "state_conv": np.ascontiguousarray(np.roll(f("state_conv")[0], -c, axis=0)),
            "pt_T": np.ascontiguousarray(np.asarray(inp["page_table"], np.int32)[c].reshape(128, 1)),
            "cos_t": np.ascontiguousarray(cos.reshape(17, 128, 8).transpose(1, 0, 2)),
            "sin_t": np.ascontiguousarray(sin.reshape(17, 128, 8).transpose(1, 0, 2)),
            "ctx_bias": np.full((128, 1), 0.0 if hf == 1 else NEG, np.float32),
        })
        maps.append(m)
    return maps


def _assemble(results):
    y = np.zeros((4, 2048, D), np.float32)
    nk = np.zeros((1, 4, 2048, 8, 128), np.float32)
    nv = np.zeros((1, 4, 2048, 8, 128), np.float32)
    ncp = np.zeros((1, 4, 30, 1024), np.float32)
    for c in range(8):
        b, hf = c // 2, c % 2
        r = results[c]
        sl = slice(hf * S_OWN, (hf + 1) * S_OWN)
        y[b, sl] = r["o_y"]
        nk[0, b, sl] = r["o_k"].reshape(S_OWN, 8, 128)
        nv[0, b, sl] = r["o_v"].reshape(S_OWN, 8, 128)
        if hf == 1:
            ncp[0, b] = r["o_conv"]
    r0 = results[0]
    ys = np.stack([np.asarray(results[c]["o_ys"], np.float32)[0] for c in range(8)]).reshape(8, 1, D)
    nks = np.asarray(r0["o_ks"], np.float32).reshape(1, 8, 1, 8, 128)
    nvs = np.asarray(r0["o_vs"], np.float32).reshape(1, 8, 1, 8, 128)
    ncs = np.asarray(r0["o_convs"], np.float32).reshape(1, 8, 30, 1024)
    return (y, ys, nk, nv, ncp, nks, nvs, ncs)


_NC_CACHE = {}


def kernel(**inp):
    if "nc" not in _NC_CACHE:
        _NC_CACHE["nc"] = build()
    nc = _NC_CACHE["nc"]
    maps = _prep_inputs(inp)
    res = run_bass_kernel_spmd(nc, maps, core_ids=list(range(8)))
    return _assemble(res.results)
```
